# Optimizing a Trainium2 kernel written in Bass

```python
import math
import jax
import jax.numpy as jnp
from jax import lax
import numpy as np

D_MODEL = 1024
BATCH = 8
SEQ = 2048
DEPTH = 2
DEC_BATCH = 16
DEC_SEQ = 16
PAST_LEN = 1024

CHUNK = 64
SGU_CHUNK = 128
Q_BLOCK = 128
MIX_WIDTH = D_MODEL
WIDTH_A = MIX_WIDTH // 2
WIDTH_B = MIX_WIDTH // 2
N_HEADS_A = 4
HEAD_DIM_A = WIDTH_A // N_HEADS_A
N_HEADS_B = 4
HEAD_DIM_V = WIDTH_B // N_HEADS_B
HEAD_DIM_QK = HEAD_DIM_V // 2
QK_WIDTH = N_HEADS_B * 2 * HEAD_DIM_QK
IN_WIDTH = 3 * WIDTH_A + 2 * QK_WIDTH + 2 * WIDTH_B
SPLIT_POINTS = (WIDTH_A, 2 * WIDTH_A, 3 * WIDTH_A,
                3 * WIDTH_A + QK_WIDTH, 3 * WIDTH_A + 2 * QK_WIDTH,
                3 * WIDTH_A + 2 * QK_WIDTH + WIDTH_B)
NORM_EPS = 1e-6
NEG_INF = -1e30

kernel_name = "hybrid_gmlp_diffattn_stream_step"


def rms_norm(x, g):
    xf = x.astype(jnp.float32)
    y = xf * lax.rsqrt(jnp.mean(xf * xf, axis=-1, keepdims=True) + NORM_EPS)
    return (y * g.astype(jnp.float32)).astype(x.dtype)


def alibi_slopes():
    h = jnp.arange(1, N_HEADS_B + 1, dtype=jnp.float32)
    return jnp.exp2(-8.0 * h / N_HEADS_B)


def spatial_gate(v, g, w_s, b_s):
    bsz, t = v.shape[0], v.shape[1]
    length = min(t, SGU_CHUNK)
    n_chunks = t // length
    vn = rms_norm(v, g).reshape(bsz, n_chunks, length, N_HEADS_A, HEAD_DIM_A)
    causal = jnp.tril(jnp.ones((length, length), dtype=w_s.dtype))
    w = w_s[:, :length, :length] * causal
    bias = jnp.transpose(b_s[:, :length])[:, :, None]
    out = jnp.einsum("hts,bnshd->bnthd", w, vn) + bias
    return out.reshape(bsz, t, N_HEADS_A, HEAD_DIM_A)


def diff_attend(q, k, v, q_pos, k_pos, lam):
    s = jnp.einsum("bqhcd,bkhcd->bchqk", q.astype(jnp.float32), k.astype(jnp.float32)) * (HEAD_DIM_QK ** -0.5)
    dist = jnp.abs(q_pos[:, None] - k_pos[None, :]).astype(jnp.float32)
    allowed = (k_pos[None, :] // CHUNK) <= (q_pos[:, None] // CHUNK)
    s = s - alibi_slopes()[None, None, :, None, None] * dist
    s = jnp.where(allowed, s, NEG_INF)
    p = jax.nn.softmax(s, axis=-1)
    a = p[:, 0] - lam * p[:, 1]
    return jnp.einsum("bhqk,bkhd->bqhd", a, v.astype(jnp.float32)).astype(v.dtype)


def mixer_layer(x, pos0, past_k, past_v, layer_idx, blocked,
                norm_g, w_in, sgu_norm_g, sgu_w, sgu_b, q_norm_g, k_norm_g,
                lambda_q1, lambda_k1, lambda_q2, lambda_k2, subln_g, w_out):
    bsz, t = x.shape[0], x.shape[1]
    h = rms_norm(x, norm_g)
    z = jnp.einsum("btd,de->bte", h, w_in)
    u_a, v_a, g_a, q, k, v, g_b = jnp.split(z, SPLIT_POINTS, axis=-1)

    y_a = u_a.reshape(bsz, t, N_HEADS_A, HEAD_DIM_A) * spatial_gate(
        v_a.reshape(bsz, t, N_HEADS_A, HEAD_DIM_A), sgu_norm_g, sgu_w, sgu_b)
    y_a = y_a.reshape(bsz, t, WIDTH_A) * jax.nn.silu(g_a)

    q = rms_norm(q.reshape(bsz, t, N_HEADS_B, 2, HEAD_DIM_QK), q_norm_g)
    k = rms_norm(k.reshape(bsz, t, N_HEADS_B, 2, HEAD_DIM_QK), k_norm_g)
    v = v.reshape(bsz, t, N_HEADS_B, HEAD_DIM_V)
    lam_init = 0.8 - 0.6 * math.exp(-0.3 * layer_idx)
    lam = (jnp.exp(jnp.sum(lambda_q1.astype(jnp.float32) * lambda_k1.astype(jnp.float32)))
           - jnp.exp(jnp.sum(lambda_q2.astype(jnp.float32) * lambda_k2.astype(jnp.float32)))
           + lam_init)
    q_pos = pos0 + jnp.arange(t, dtype=jnp.int32)
    if past_k is None:
        k_all, v_all, k_pos = k, v, q_pos
    else:
        k_all = jnp.concatenate([past_k.astype(k.dtype), k], axis=1)
        v_all = jnp.concatenate([past_v.astype(v.dtype), v], axis=1)
        k_pos = jnp.arange(k_all.shape[1], dtype=jnp.int32)
    if blocked:
        n_blocks = t // Q_BLOCK
        q_blocks = jnp.moveaxis(q.reshape(bsz, n_blocks, Q_BLOCK, N_HEADS_B, 2, HEAD_DIM_QK), 1, 0)
        pos_blocks = q_pos.reshape(n_blocks, Q_BLOCK)
        o = lax.map(lambda args: diff_attend(args[0], k_all, v_all, args[1], k_pos, lam),
                    (q_blocks, pos_blocks))
        o = jnp.moveaxis(o, 0, 1).reshape(bsz, t, N_HEADS_B, HEAD_DIM_V)
    else:
        o = diff_attend(q, k_all, v_all, q_pos, k_pos, lam)
    o = rms_norm(o, subln_g) * (1.0 - lam_init)
    y_b = o.reshape(bsz, t, WIDTH_B) * jax.nn.silu(g_b)

    y = jnp.einsum("bte,ed->btd", jnp.concatenate([y_a, y_b], axis=-1), w_out)
    return (x + y, k, v, v_a)


def setup_inputs(seed: int = 0) -> dict:
    key = jax.random.key(seed)
    ks = jax.random.split(key, 18)
    f32 = jnp.float32

    def nrm(k, shape, scale):
        return scale * jax.random.normal(k, shape, f32)

    return {
        "x_prompt": nrm(ks[0], (BATCH, SEQ, D_MODEL), 1.0),
        "x_sample": nrm(ks[1], (DEC_BATCH, DEC_SEQ, D_MODEL), 1.0),
        "cache_k": nrm(ks[2], (DEPTH, DEC_BATCH, PAST_LEN, N_HEADS_B, 2, HEAD_DIM_QK), 1.0),
        "cache_v": nrm(ks[3], (DEPTH, DEC_BATCH, PAST_LEN, N_HEADS_B, HEAD_DIM_V), 1.0),
        "norm_g": 1.0 + nrm(ks[4], (DEPTH, D_MODEL), 0.05),
        "w_in": nrm(ks[5], (DEPTH, D_MODEL, IN_WIDTH), D_MODEL ** -0.5),
        "sgu_norm_g": 1.0 + nrm(ks[6], (DEPTH, N_HEADS_A, HEAD_DIM_A), 0.05),
        "sgu_w": nrm(ks[7], (DEPTH, N_HEADS_A, SGU_CHUNK, SGU_CHUNK), 0.5 * SGU_CHUNK ** -0.5),
        "sgu_b": 1.0 + nrm(ks[8], (DEPTH, N_HEADS_A, SGU_CHUNK), 0.1),
        "q_norm_g": 1.0 + nrm(ks[9], (DEPTH, HEAD_DIM_QK), 0.05),
        "k_norm_g": 1.0 + nrm(ks[10], (DEPTH, HEAD_DIM_QK), 0.05),
        "lambda_q1": nrm(ks[11], (DEPTH, HEAD_DIM_QK), 0.1),
        "lambda_k1": nrm(ks[12], (DEPTH, HEAD_DIM_QK), 0.1),
        "lambda_q2": nrm(ks[13], (DEPTH, HEAD_DIM_QK), 0.1),
        "lambda_k2": nrm(ks[14], (DEPTH, HEAD_DIM_QK), 0.1),
        "subln_g": 1.0 + nrm(ks[15], (DEPTH, HEAD_DIM_V), 0.05),
        "w_out": nrm(ks[16], (DEPTH, MIX_WIDTH, D_MODEL), MIX_WIDTH ** -0.5),
    }


def reference(x_prompt, x_sample, cache_k, cache_v, norm_g, w_in, sgu_norm_g, sgu_w, sgu_b,
              q_norm_g, k_norm_g, lambda_q1, lambda_k1, lambda_q2, lambda_k2, subln_g, w_out):
    xp, xs = x_prompt, x_sample
    kp_rows, vp_rows, ks_rows, vs_rows, sgu_rows = [], [], [], [], []
    for i in range(DEPTH):
        params = (norm_g[i], w_in[i], sgu_norm_g[i], sgu_w[i], sgu_b[i], q_norm_g[i], k_norm_g[i],
                  lambda_q1[i], lambda_k1[i], lambda_q2[i], lambda_k2[i], subln_g[i], w_out[i])
        xp, kp, vp, _ = mixer_layer(xp, 0, None, None, i, True, *params)
        xs, kn, vn, va = mixer_layer(xs, PAST_LEN, cache_k[i], cache_v[i], i, False, *params)
        kp_rows.append(kp)
        vp_rows.append(vp)
        ks_rows.append(kn)
        vs_rows.append(vn)
        sgu_rows.append(va)
    new_k_prompt = jnp.stack(kp_rows)
    new_v_prompt = jnp.stack(vp_rows)
    new_k_sample = jnp.stack(ks_rows)
    new_v_sample = jnp.stack(vs_rows)
    new_sgu_v_sample = jnp.stack(sgu_rows)
    return (xp, xs, new_k_prompt, new_v_prompt, new_k_sample, new_v_sample, new_sgu_v_sample)
```

```python
import math
from contextlib import ExitStack

import numpy as np
import concourse.bass as bass
import concourse.mybir as mybir
from concourse.bass_utils import run_bass_kernel_spmd

F32 = mybir.dt.float32
BF16 = mybir.dt.bfloat16
I32 = mybir.dt.int32
AF = mybir.ActivationFunctionType
ALU = mybir.AluOpType
AX = mybir.AxisListType

ENGS = ("pe", "act", "dve", "pool", "sp")
_SBUF_FREE = [0]
OPT = dict((('TH_IN_N', 0), ('YIELD_YT', 1), ('YIELD_C0', 1), ('YIELD_QT', 0), ('LAG', 1), ('PAIR', 1), ('YIELD_TH', 1), ('C_POOL', 0), ('VA_POOL', 0), ('ORDER', 0), ('COPY_DVE_T', 99), ('EVAC_ACT_T', 0)))
NDMA_SEM = 8

D_MODEL = 1024
IN_W = 3584
NH = 4
EPS = 1e-6
PAST = 1024
DEC = 16
SLOPES = [2.0 ** (-8.0 * (h + 1) / NH) for h in range(NH)]
NEG = -30000.0
C_U, C_VA, C_GA, C_Q, C_K, C_V, C_GB = range(7)


class Op:
    __slots__ = ("eng", "fn", "dma", "deps", "sig", "signum", "qidx")

    def __init__(self, eng, fn, dma):
        self.eng = eng
        self.fn = fn
        self.dma = dma
        self.deps = set()
        self.sig = False
        self.signum = None
        self.qidx = None


class _Rec:
    def __init__(self):
        self.call = None

    def __getattr__(self, name):
        def f(*a, **k):
            self.call = (name, a, k)
            return self
        return f


class Prog:
    def __init__(self):
        self.ops = {e: [] for e in ENGS}
        self.res = {}
        self.dma_ops = {e: [] for e in ENGS}
        self.final_waits = []
        self.prepared = False
        self.nwaits = 0

    def _st(self, key):
        st = self.res.get(key)
        if st is None:
            st = [None, []]
            self.res[key] = st
        return st

    def op(self, eng, fn, reads=(), writes=(), dma=False):
        rec = _Rec()
        fn(rec)
        assert rec.call is not None
        o = Op(eng, rec.call, dma)
        deps = set()
        for r in reads:
            st = self._st(r)
            if st[0] is not None:
                deps.add((st[0], 0))
        for w in writes:
            st = self._st(w)
            if st[0] is not None:
                deps.add((st[0], 1))
            for rd in st[1]:
                deps.add((rd, 2))
        for key in tuple(reads) + tuple(writes):
            if isinstance(key, tuple) and key and key[0] == "P":
                st = self._st(key)
                for rd in st[1]:
                    if rd.eng != eng:
                        deps.add((rd, 3))
        for d, kind in deps:
            if d is o:
                continue
            if (not d.dma) and (not dma) and d.eng == eng and eng == "pe":
                continue
            o.deps.add(d)
        if dma:
            q = self.dma_ops[eng]
            o.qidx = len(q)
            if o.qidx >= NDMA_SEM:
                o.deps.add(q[o.qidx - NDMA_SEM])
            q.append(o)
        for r in reads:
            self._st(r)[1].append(o)
        for w in writes:
            st = self._st(w)
            st[0] = o
            st[1] = []
        self.ops[eng].append(o)
        return o

    def dma(self, eng, out, in_, reads=(), writes=(), final=False, **kw):
        def fn(e):
            return e.dma_start(out=out, in_=in_, **kw)
        o = self.op(eng, fn, reads, writes, dma=True)
        if final:
            self.final_waits.append(o)
        return o

    def prepare(self):
        for e in ENGS:
            for o in self.ops[e]:
                for d in o.deps:
                    d.sig = True
        for o in self.final_waits:
            o.sig = True
        for e in ENGS:
            n = 0
            for o in self.ops[e]:
                if o.dma:
                    continue
                if o.sig:
                    n += 1
                    o.signum = n
        self.prepared = True

    def emit_engine(self, e, eng, sems, dsems):
        if not self.prepared:
            self.prepare()

        def target(d):
            if d.dma:
                return (dsems[d.eng][d.qidx % NDMA_SEM], 16 * (d.qidx // NDMA_SEM + 1))
            return (sems[d.eng], d.signum)

        wm = {}
        for o in self.ops[e]:
            need = {}
            for d in o.deps:
                s, v = target(d)
                k = id(s)
                if wm.get(k, 0) >= v:
                    continue
                if k not in need or need[k][1] < v:
                    need[k] = (s, v)
            for k, (s, v) in need.items():
                eng.wait_ge(s, v)
                wm[k] = v
                self.nwaits += 1
            name_, a_, k_ = o.fn
            ins = getattr(eng, name_)(*a_, **k_)
            if o.dma:
                ins.then_inc(dsems[e][o.qidx % NDMA_SEM], 16)
            elif o.sig:
                ins.then_inc(sems[e], 1)
        if e == "sp":
            for o in self.final_waits:
                s, v = target(o)
                if wm.get(id(s), 0) >= v:
                    continue
                eng.wait_ge(s, v)
                wm[id(s)] = v


def _bf16_round(x):
    u = np.array([x], dtype=np.float32).view(np.uint32)
    u = ((u + np.uint32(0x7FFF) + ((u >> np.uint32(16)) & np.uint32(1))) & np.uint32(0xFFFF0000)).astype(np.uint32)
    return float(u.view(np.float32)[0])


def _bf16_split3(c):
    hi = _bf16_round(c)
    mid = _bf16_round(c - hi)
    lo = _bf16_round(c - hi - mid)
    return hi, mid, lo


class Rot:
    def __init__(self, alloc, name, n, shape, dt):
        self.bufs = [alloc(f"{name}{i}", shape, dt) for i in range(n)]
        self.name = name
        self.i = -1

    def next(self):
        self.i = (self.i + 1) % len(self.bufs)
        return self.bufs[self.i], (self.name, self.i)


def build_program(T=2048, L=2, NS=2, interleave=True):
    NT = T // 128
    nc = bass.Bass("TRN2", target_bir_lowering=False)

    def din(name, shape):
        return nc.dram_tensor(name, list(shape), F32, kind="ExternalInput").ap()

    def dout(name, shape):
        return nc.dram_tensor(name, list(shape), F32, kind="ExternalOutput").ap()

    xp = din("xp", [T, D_MODEL])
    xs = din("xs", [NS, DEC, D_MODEL])
    ck = din("ck", [L, NS, PAST, 512])
    cv = din("cv", [L, NS, PAST, 512])
    norm_g = din("norm_g", [L, D_MODEL])
    w_in = din("w_in", [L, D_MODEL, IN_W])
    sgu_norm_g = din("sgu_norm_g", [L, 512])
    sgu_w = din("sgu_w", [L, NH, 128, 128])
    sgu_b = din("sgu_b", [L, NH, 128])
    q_norm_g = din("q_norm_g", [L, 64])
    k_norm_g = din("k_norm_g", [L, 64])
    lq1 = din("lq1", [L, 64])
    lk1 = din("lk1", [L, 64])
    lq2 = din("lq2", [L, 64])
    lk2 = din("lk2", [L, 64])
    subln_g = din("subln_g", [L, 128])
    w_out = din("w_out", [L, D_MODEL, D_MODEL])

    yp = dout("yp", [T, D_MODEL])
    ys = dout("ys", [NS, DEC, D_MODEL])
    nkp = dout("nkp", [L, T, 512])
    nvp = dout("nvp", [L, T, 512])
    nks = dout("nks", [L, NS, DEC, 512])
    nvs = dout("nvs", [L, NS, DEC, 512])
    nsg = dout("nsg", [L, NS, DEC, 512])
    xmid = nc.dram_tensor("xmid", [T, D_MODEL], F32, kind="Internal").ap()
    xsmid = nc.dram_tensor("xsmid", [NS, DEC, D_MODEL], F32, kind="Internal").ap()

    P = Prog()
    es = ExitStack()

    def sb(name, shape, dt=F32):
        return es.enter_context(nc.sbuf_tensor(name, list(shape), dt))

    def ps(name, shape, dt=F32):
        return es.enter_context(nc.psum_tensor(name, list(shape), dt))

    win = sb("win", [128, 8, IN_W], BF16)
    wout = sb("wout", [128, 8, D_MODEL], BF16)
    KT = sb("KT", [128, NH, T], BF16)
    VA = sb("VA", [128, NT, NH, 132], BF16)
    ident = sb("ident", [128, 128], BF16)
    Dt = sb("Dt", [128, NH, 128], BF16)
    mhalf = sb("mhalf", [128, 8], F32)
    vsc = sb("vsc", [128, NH], F32)
    btab0 = sb("btab0", [128, NH, 16], F32)
    btab = sb("btab", [128, NH, 16], F32)
    bsx0 = sb("bsx0", [128, NH, 8, DEC], F32)
    gN = sb("gN", [128, D_MODEL], F32)
    gS = sb("gS", [128, 512], F32)
    gqc = sb("gqc", [128, 1], F32)
    gkc = sb("gkc", [128, 1], F32)
    gl = sb("gl", [128, 128], F32)
    WT = sb("WT", [128, NH, 128], BF16)
    bS = sb("bS", [128, NH], F32)
    nlam = sb("nlam", [128, 1], F32)
    Mb = sb("Mb", [128, 1], F32)
    small = sb("small", [128, 7, 64], F32)
    gLr = sb("gLr", [128, 128], F32)
    sc = sb("sc", [128, 16], F32)
    negM = sb("negM", [128, 1], F32)

    xt = Rot(sb, "xt", 4, [128, D_MODEL], F32)
    hb = Rot(sb, "hb", 2, [128, D_MODEL], BF16)
    hT = Rot(sb, "hT", 2, [128, 8, 128], BF16)
    sq = Rot(sb, "sq", 2, [128, 512], F32)
    zr = Rot(sb, "zr", 3, [128, 512], F32)
    stat = Rot(sb, "stat", 8, [128, 8], F32)
    rst = Rot(sb, "rst", 8, [128, 8], F32)
    ta = Rot(sb, "ta", 1, [128, 512], F32)
    van = Rot(sb, "van", 1, [128, 512], BF16)
    qn = Rot(sb, "qn", 1, [128, 512], BF16)
    knb = Rot(sb, "knb", 1, [128, 512], BF16)
    vf = Rot(sb, "vf", 1, [128, 512], F32)
    QT = Rot(sb, "QT", 2, [128, NH, 128], BF16)
    tb = Rot(sb, "tb", 3, [128, 512], F32)
    PT = Rot(sb, "PT", 6, [128, 2, 256], BF16) if OPT["PAIR"] else Rot(sb, "PT", 12, [128, 2, 128], BF16)
    ob = Rot(sb, "ob", 2, [128, 512], F32)
    t1 = Rot(sb, "t1", 2, [128, 128], F32)
    r2 = Rot(sb, "r2", 4, [128, 4], F32)
    yb = Rot(sb, "yb", 3, [128, D_MODEL], BF16)
    yT = Rot(sb, "yT", 1, [128, 8, 128], BF16)
    xo = Rot(sb, "xo", 1, [128, D_MODEL], F32)
    Kc = Rot(sb, "Kc", 2, [128, 8, 128], BF16)
    KTs = Rot(sb, "KTs", 1, [128, 8 * 128], BF16)
    Vs = Rot(sb, "Vs", 2, [128, 8, 130], BF16)
    KTn = Rot(sb, "KTn", 2, [128, NH, DEC], BF16)
    Vn = Rot(sb, "Vn", 2, [DEC, NH, 130], BF16)
    ssb = Rot(sb, "ssb", 2, [128, 2, 128], F32)
    PTs = Rot(sb, "PTs", 2, [128, 2, 128], BF16)
    PTn = Rot(sb, "PTn", 2, [DEC, 2, DEC], BF16)

    wf32 = sq.bufs[0][:].rearrange("p (h s) -> p h s", h=NH)
    wbf_t = sb("wbf_t", [128, NH * 128], BF16)
    wbf = wbf_t[:].rearrange("p (h s) -> p h s", h=NH)
    WF = ("sq", 0)
    WB = "wbf_t"
    class PRot:
        def __init__(self, tag, n, shape):
            self.bufs = [ps(f"psum_{tag}{i}", shape, F32) for i in range(n)]
            self.tag = tag
            self.i = -1

        def next(self):
            self.i = (self.i + 1) % len(self.bufs)
            return self.bufs[self.i], ("P", self.tag, self.i)

    zp = PRot("zb", 2, [128, 512])
    su = PRot("su", 2, [128, 2, 512])
    opb = PRot("ob", 2, [128, 512])

    class TP:
        def next(self):
            z_, zk = zp.next()
            return z_[:].bitcast(BF16), zk

    tp = TP()

    _SBUF_FREE[0] = nc.sbuf_bytes_remaining
    def act(fn, r=(), w=()):
        return P.op("act", fn, r, w)

    def dve(fn, r=(), w=()):
        return P.op("dve", fn, r, w)

    def pool(fn, r=(), w=()):
        return P.op("pool", fn, r, w)

    def pe(fn, r=(), w=()):
        return P.op("pe", fn, r, w)

    def setup_min():
        idf = wf32
        pool(lambda e: e.memset(mhalf[:], -0.5), w=["mhalf"])
        pool(lambda e: e.memset(idf[:, 0, :], 0.0), w=[WF])
        pool(lambda e: e.affine_select(out=idf[:, 0, :], in_=idf[:, 0, :], pattern=[[-1, 128]],
                                       compare_op=ALU.not_equal, fill=1.0, base=0, channel_multiplier=1),
             r=[WF], w=[WF])
        dve(lambda e: e.tensor_copy(ident[:], idf[:, 0, :]), r=[WF], w=["ident"])

    def setup_consts():
        idf = wf32
        pool(lambda e: e.memset(VA[:, :, :, 128:132], 0.0), w=[("VA", j) for j in range(NT)])
        pool(lambda e: e.memset(VA[:, :, :, 128:129], 1.0), w=[("VA", j) for j in range(NT)])
        for h in range(NH):
            hi, mid, lo = _bf16_split3(math.exp(128.0 * SLOPES[h]))
            cval = float(np.float32(np.float32(hi) + np.float32(mid) + np.float32(lo)))
            pool(lambda e, h=h, cval=cval: e.memset(vsc[:, h:h + 1], cval), w=["vsc"])
            for j in (range(1, NT, 2) if OPT['PAIR'] else ()):
                for ci, cv_ in enumerate((hi, mid, lo)):
                    pool(lambda e, h=h, j=j, ci=ci, cv_=cv_: e.memset(VA[:, j, h, 128 + ci:129 + ci], cv_),
                         w=[("VA", j)])
        for i in range(2):
            b_ = Vs.bufs[i]
            pool(lambda e, b_=b_: e.memset(b_[:, :, 128:130], 1.0), w=[("Vs", i)])
        for i in range(2):
            b_ = Vn.bufs[i]
            pool(lambda e, b_=b_: e.memset(b_[:, :, 128:130], 1.0), w=[("Vn", i)])
        pool(lambda e: e.iota(idf[:, 1, :], pattern=[[-1, 128]], base=0, channel_multiplier=1,
                              allow_small_or_imprecise_dtypes=True), r=["ident"], w=[WF])
        for h in range(NH):
            dve(lambda e, h=h: e.tensor_scalar(out=idf[:, 2, :], in0=idf[:, 1, :], scalar1=0.0,
                                               scalar2=-2.0 * SLOPES[h], op0=ALU.max, op1=ALU.mult),
                r=[WF], w=[WF])
            dve(lambda e: e.memset(idf[64:128, 2, 0:64], NEG), r=[WF], w=[WF])
            dve(lambda e, h=h: e.tensor_copy(Dt[:, h, 0:128], idf[:, 2, :]), r=[WF], w=["Dt"])
        pool(lambda e: e.iota(idf[:, 3, 0:16], pattern=[[-128, 16]], base=0, channel_multiplier=1,
                              allow_small_or_imprecise_dtypes=True), r=["Dt"], w=[WF])
        for h in range(NH):
            dve(lambda e, h=h: e.tensor_scalar(out=btab0[:, h, :], in0=idf[:, 3, 0:16], scalar1=SLOPES[h],
                                               scalar2=None, op0=ALU.mult), r=[WF], w=["btab0"])
        pool(lambda e: e.iota(idf[:, 3, :].rearrange("p (j q) -> p j q", q=DEC), pattern=[[128, 8], [0, DEC]],
                              base=-PAST, channel_multiplier=1, allow_small_or_imprecise_dtypes=True),
             r=["btab0"], w=[WF])
        for h in range(NH):
            dve(lambda e, h=h: e.tensor_scalar(out=bsx0[:, h, :, :].rearrange("p j q -> p (j q)"),
                                               in0=idf[:, 3, :], scalar1=SLOPES[h], scalar2=None, op0=ALU.mult),
                r=[WF], w=["bsx0"])

    CHUNK_ORDER = [C_Q, C_K, C_V, C_GA, C_VA, C_U, C_GB]

    def load_weights(l):
        src = w_in[l].rearrange("(kt p) n -> p kt n", p=128)
        for c in CHUNK_ORDER:
            P.dma("pool", win[:, :, c * 512:(c + 1) * 512], src[:, :, c * 512:(c + 1) * 512],
                  writes=[("win", c)])
        src2 = w_out[l].rearrange("(kt p) n -> p kt n", p=128)
        for n in range(2):
            P.dma("pool", wout[:, :, n * 512:(n + 1) * 512], src2[:, :, n * 512:(n + 1) * 512],
                  writes=[("wout", n)])

    def load_params_n(l):
        P.dma("sp", gN[:], norm_g[l:l + 1, :].partition_broadcast(128), writes=["gN"])

    def load_params_a(l):
        lam_init = 0.8 - 0.6 * math.exp(-0.3 * l)
        P.dma("sp", gS[:], sgu_norm_g[l:l + 1, :].partition_broadcast(128), writes=["gS"])
        for i, t_ in enumerate((q_norm_g, k_norm_g)):
            P.dma("sp", small[:, i, :], t_[l:l + 1, :].partition_broadcast(128), writes=[("small", i)])
        P.dma("sp", gLr[:], subln_g[l:l + 1, :].partition_broadcast(128), writes=["gLr"])
        for half in range(2):
            P.dma("sp", gqc[half * 64:(half + 1) * 64, :], q_norm_g[l].rearrange("(d o) -> d o", o=1), writes=["gqc"],
                  allow_slow_non_contiguous=True)
            P.dma("sp", gkc[half * 64:(half + 1) * 64, :], k_norm_g[l].rearrange("(d o) -> d o", o=1), writes=["gkc"],
                  allow_slow_non_contiguous=True)
        dve(lambda e: e.tensor_scalar(out=gqc[:], in0=gqc[:], scalar1=0.125, scalar2=None, op0=ALU.mult),
            r=["gqc"], w=["gqc"])
        dve(lambda e: e.tensor_scalar(out=gl[:], in0=gLr[:], scalar1=0.5 * (1.0 - lam_init), scalar2=None,
                                      op0=ALU.mult), r=["gLr"], w=["gl"])
        dve(lambda e: e.tensor_reduce(out=sc[:, 0:1], in_=small[:, 0, :], axis=AX.X, op=ALU.max,
                                      apply_absolute_value=True), r=[("small", 0)], w=["sc0"])
        dve(lambda e: e.tensor_reduce(out=sc[:, 1:2], in_=small[:, 1, :], axis=AX.X, op=ALU.max,
                                      apply_absolute_value=True), r=[("small", 1)], w=["sc1"])
        dve(lambda e: e.tensor_scalar(out=Mb[:], in0=sc[:, 0:1], scalar1=sc[:, 1:2], scalar2=8.0,
                                      op0=ALU.mult, op1=ALU.mult), r=["sc0", "sc1"], w=["Mb"])
    def load_params_a2(l):
        P.dma("sp", wf32[:], sgu_w[l].rearrange("h t s -> t h s"), writes=[WF])
        P.dma("sp", bS[:], sgu_b[l].rearrange("h t -> t h"), writes=["bS"], allow_slow_non_contiguous=True)
        for h in range(NH):
            pool(lambda e, h=h: e.affine_select(out=wf32[:, h, :], in_=wf32[:, h, :], pattern=[[-1, 128]],
                                                compare_op=ALU.is_ge, fill=0.0, base=0, channel_multiplier=1),
                 r=[WF], w=[WF])
        dve(lambda e: e.tensor_scalar(out=wbf[:].rearrange("p h s -> p (h s)"),
                                      in0=wf32[:].rearrange("p h s -> p (h s)"), scalar1=0.5, scalar2=None,
                                      op0=ALU.mult), r=[WF], w=[WB])
        tpt, tpk = tp.next()
        for h in range(NH):
            pe(lambda e, h=h: e.transpose(tpt[:, h * 128:(h + 1) * 128], wbf[:, h, :], ident[:]),
               r=[WB, "ident"], w=[tpk])
        dve(lambda e: e.tensor_copy(WT[:].rearrange("p h t -> p (h t)"), tpt[:, 0:512]), r=[tpk], w=["WT"])
        dve(lambda e: e.tensor_scalar(out=bS[:], in0=bS[:], scalar1=0.5, scalar2=None, op0=ALU.mult),
            r=["bS"], w=["bS"])

    def load_params_b(l):
        lam_init = 0.8 - 0.6 * math.exp(-0.3 * l)
        dve(lambda e: e.tensor_scalar(out=btab[:].rearrange("p h d -> p (h d)"),
                                      in0=btab0[:].rearrange("p h d -> p (h d)"), scalar1=Mb[:, 0:1],
                                      scalar2=None, op0=ALU.subtract), r=["Mb", "btab0"], w=["btab"])
        dve(lambda e: e.tensor_scalar(out=negM[:], in0=Mb[:], scalar1=-1.0, scalar2=None, op0=ALU.mult),
            r=["Mb"], w=["negM"])
        for i, (a_, b_) in enumerate(((lq1, lk1), (lq2, lk2))):
            sa, sb_ = 2 + 2 * i, 3 + 2 * i
            P.dma("sp", small[:, sa, :], a_[l:l + 1, :].partition_broadcast(128), writes=[("small", sa)])
            P.dma("sp", small[:, sb_, :], b_[l:l + 1, :].partition_broadcast(128), writes=[("small", sb_)])
            dve(lambda e, sa=sa, sb_=sb_: e.tensor_tensor(out=small[:, 6, :], in0=small[:, sa, :], in1=small[:, sb_, :],
                                                          op=ALU.mult),
                r=[("small", sa), ("small", sb_)], w=[("small", 6)])
            dve(lambda e, i=i: e.tensor_reduce(out=sc[:, 2 + i:3 + i], in_=small[:, 6, :], axis=AX.X, op=ALU.add),
                r=[("small", 6)], w=[f"sc{2 + i}"])
            act(lambda e, i=i: e.activation(out=sc[:, 4 + i:5 + i], in_=sc[:, 2 + i:3 + i], func=AF.Exp),
                r=[f"sc{2 + i}"], w=[f"sc{4 + i}"])
        dve(lambda e: e.tensor_scalar(out=nlam[:], in0=sc[:, 5:6], scalar1=sc[:, 4:5], scalar2=-lam_init,
                                      op0=ALU.subtract, op1=ALU.add), r=["sc4", "sc5"], w=["nlam"])

    def rsqrt_cols(src_ap, n, np_, scale, rkeys):
        st_, stk = stat.next()
        rs_, rsk = rst.next()
        pool(lambda e: e.tensor_scalar(out=st_[0:np_, 0:n], in0=src_ap, scalar1=scale, scalar2=EPS,
                                       op0=ALU.mult, op1=ALU.add), r=rkeys, w=[stk])
        pool(lambda e: e.tensor_tensor(out=rs_[0:np_, 0:n], in0=st_[0:np_, 0:n], in1=mhalf[0:np_, 0:n], op=ALU.pow),
             r=[stk, "mhalf"], w=[rsk])
        return rs_[0:np_, 0:n], rsk

    copy_on_dve = [False]
    evac_on_act = [False]

    def evac(out_ap, in_ap, rkeys, wkeys, scale_ap=None, scale_key=None):
        if evac_on_act[0]:
            if scale_ap is None:
                act(lambda e: e.activation(out=out_ap, in_=in_ap, func=AF.Copy), r=rkeys, w=wkeys)
            else:
                act(lambda e: e.activation(out=out_ap, in_=in_ap, func=AF.Copy, scale=scale_ap), r=rkeys + [scale_key], w=wkeys)
        else:
            if scale_ap is None:
                try:
                    oi, ii = out_ap.bitcast(I32), in_ap.bitcast(I32)
                except Exception:
                    oi, ii = out_ap, in_ap
                dve(lambda e: e.tensor_copy(oi, ii), r=rkeys, w=wkeys)
            else:
                dve(lambda e: e.tensor_scalar(out=out_ap, in0=in_ap, scalar1=scale_ap, scalar2=None, op0=ALU.mult),
                    r=rkeys + [scale_key], w=wkeys)

    def gn_stats(z_, zk, np_, ng, gs):
        r_, rk = zr.next()
        if copy_on_dve[0]:
            dve(lambda e: e.tensor_copy(r_[0:np_, :], z_[0:np_, :]), r=[zk], w=[rk])
        else:
            act(lambda e: e.activation(out=r_[0:np_, :], in_=z_[0:np_, :], func=AF.Copy), r=[zk], w=[rk])
        s_, sk = sq.next()
        act(lambda e: e.activation(out=s_[0:np_, :], in_=z_[0:np_, :], func=AF.Square), r=[zk], w=[sk])
        st_, stk = stat.next()
        dve(lambda e: e.tensor_reduce(out=st_[0:np_, 0:ng], in_=s_[0:np_, :].rearrange("p (a b) -> p a b", b=gs),
                                      axis=AX.X, op=ALU.add), r=[sk], w=[stk])
        rs_ap, rsk = rsqrt_cols(st_[0:np_, 0:ng], ng, np_, 1.0 / gs, [stk])
        return r_, rk, rs_ap, rsk

    def gn_apply(eng_fn, zr_, zrk, rs_ap, rsk, np_, ng, gs, out_ap, out_keys):
        eng_fn(lambda e: e.tensor_tensor(out=out_ap.rearrange("p (a b) -> p a b", b=gs),
                                         in0=zr_[0:np_, :].rearrange("p (a b) -> p a b", b=gs),
                                         in1=rs_ap.unsqueeze(2).to_broadcast([np_, ng, gs]), op=ALU.mult),
               r=[zrk, rsk], w=out_keys)

    def stage_n(np_, X, Xk):
        st_, stk = stat.next()
        h_, hk = hb.next()
        act(lambda e: e.activation(out=h_[0:np_, :], in_=X, func=AF.Square, accum_out=st_[0:np_, 0:1]),
            r=[Xk], w=[hk, stk])
        rs_ap, rsk = rsqrt_cols(st_[0:np_, 0:1], 1, np_, 1.0 / D_MODEL, [stk])

        def apply():
            dve(lambda e: e.scalar_tensor_tensor(out=h_[0:np_, :], in0=X, scalar=rs_ap, in1=gN[0:np_, :],
                                                 op0=ALU.mult, op1=ALU.mult), r=[Xk, rsk, "gN"], w=[hk])
        if OPT['LAG']:
            return h_, hk, apply
        apply()
        return h_, hk, None

    def stage_th(np_, h_, hk):
        tpt, tpk = tp.next()
        for k in range(8):
            pe(lambda e, k=k: e.transpose(tpt[:, k * np_:(k + 1) * np_], h_[0:np_, k * 128:(k + 1) * 128],
                                          ident[0:np_, 0:np_]), r=[hk, "ident"], w=[tpk])
        hT_, hTk = hT.next()
        evac(hT_[:, :, 0:np_], tpt[:, 0:8 * np_].rearrange("p (k t) -> p k t", t=np_), [tpk], [hTk])
        return hT_, hTk

    def stage_a(l, np_, h_, hk, kt_dst, va_dst, k_out, v_out, sgu_out=None, after_proj=None, mid_hook=None, hT_pre=None):
        if hT_pre is None:
            hT_, hTk = stage_th(np_, h_, hk)
            yield
            if OPT['YIELD_TH']:
                yield
        else:
            hT_, hTk = hT_pre

        def proj(c):
            z_, zk = zp.next()
            for k in range(8):
                pe(lambda e, k=k: e.matmul(z_[0:np_, :], lhsT=hT_[:, k, 0:np_], rhs=win[:, k, c * 512:(c + 1) * 512],
                                           start=(k == 0), stop=(k == 7)), r=[hTk, ("win", c)], w=[zk])
            if after_proj is not None:
                after_proj(c)
            return z_, zk

        lag = OPT['LAG']
        z_, zk = proj(C_Q)
        rq_, rqk, rsq_ap, rsqk = gn_stats(z_, zk, np_, 8, 64)
        q_, qk = qn.next()

        def apply_q():
            gn_apply(dve, rq_, rqk, rsq_ap, rsqk, np_, 8, 64, q_[0:np_, :], [qk])
        if not lag:
            apply_q()
        yield
        if lag:
            apply_q()
        z_, zk = proj(C_K)
        rk_, rkk, rsk_ap, rskk = gn_stats(z_, zk, np_, 8, 64)
        kb_, kbk = knb.next()

        def apply_k():
            gn_apply(dve, rk_, rkk, rsk_ap, rskk, np_, 8, 64, kb_[0:np_, :], [kbk])
            gn_apply(pool, rk_, rkk, rsk_ap, rskk, np_, 8, 64, rk_[0:np_, :], [rkk])
            pool(lambda e: e.tensor_tensor(out=rk_[0:np_, :].rearrange("p (a b) -> p a b", b=64),
                                           in0=rk_[0:np_, :].rearrange("p (a b) -> p a b", b=64),
                                           in1=small[0:np_, 1, :].unsqueeze(1).to_broadcast([np_, 8, 64]), op=ALU.mult),
                 r=[rkk, ("small", 1)], w=[rkk])
            P.dma("sp", k_out, rk_[0:np_, :], reads=[rkk], final=True)
        if not lag:
            apply_k()
        yield
        if lag:
            apply_k()
        z_, zk = proj(C_V)
        v_, vk = vf.next()
        if copy_on_dve[0]:
            dve(lambda e: e.tensor_copy(v_[0:np_, :], z_[0:np_, :]), r=[zk], w=[vk])
        else:
            act(lambda e: e.activation(out=v_[0:np_, :], in_=z_[0:np_, :], func=AF.Copy), r=[zk], w=[vk])
        P.dma("sp", v_out, v_[0:np_, :], reads=[vk], final=True)
        va_ap, va_keys = va_dst[0], va_dst[1]
        if len(va_dst) > 2 and va_dst[2]:
            dve(lambda e: e.tensor_tensor(out=va_ap, in0=v_[0:np_, :].rearrange("p (h d) -> p h d", d=128),
                                          in1=vsc[0:np_, :].unsqueeze(2).to_broadcast([np_, NH, 128]), op=ALU.mult),
                r=[vk, "vsc"], w=va_keys)
        else:
            dve(lambda e: e.tensor_copy(va_ap, v_[0:np_, :].rearrange("p (h d) -> p h d", d=128)), r=[vk], w=va_keys)
        yield
        z_, zk = proj(C_GA)
        ta_, tak = ta.next()
        act(lambda e: e.activation(out=ta_[0:np_, :], in_=z_[0:np_, :], func=AF.Tanh, scale=0.5), r=[zk], w=[tak])
        dve(lambda e: e.scalar_tensor_tensor(out=ta_[0:np_, :], in0=ta_[0:np_, :], scalar=1.0, in1=z_[0:np_, :],
                                             op0=ALU.add, op1=ALU.mult), r=[tak, zk], w=[tak])
        yield
        z_, zk = proj(C_VA)
        rv_, rvk, rsv_ap, rsvk = gn_stats(z_, zk, np_, 4, 128)
        if sgu_out is not None:
            P.dma("sp", sgu_out, rv_[0:np_, :], reads=[rvk], final=True)
        vn_, vnk = van.next()

        def apply_va():
            gn_apply(pool if OPT['VA_POOL'] else dve, rv_, rvk, rsv_ap, rsvk, np_, 4, 128, rv_[0:np_, :], [rvk])
            pool(lambda e: e.tensor_tensor(out=vn_[0:np_, :], in0=rv_[0:np_, :], in1=gS[0:np_, :], op=ALU.mult),
                 r=[rvk, "gS"], w=[vnk])
        if not lag:
            apply_va()
        yield
        if lag:
            apply_va()
        z_, zk = proj(C_U)
        dve(lambda e: e.tensor_tensor(out=ta_[0:np_, :], in0=z_[0:np_, :], in1=ta_[0:np_, :], op=ALU.mult),
            r=[zk, tak], w=[tak])
        yield
        z_, zk = proj(C_GB)
        tb_, tbk = tb.next()
        act(lambda e: e.activation(out=tb_[0:np_, :], in_=z_[0:np_, :], func=AF.Tanh, scale=0.5), r=[zk], w=[tbk])
        dve(lambda e: e.scalar_tensor_tensor(out=tb_[0:np_, :], in0=tb_[0:np_, :], scalar=1.0, in1=z_[0:np_, :],
                                             op0=ALU.add, op1=ALU.mult), r=[tbk, zk], w=[tbk])
        pool(lambda e: e.tensor_tensor(out=tb_[0:np_, :].rearrange("p (a b) -> p a b", b=128),
                                       in0=tb_[0:np_, :].rearrange("p (a b) -> p a b", b=128),
                                       in1=gl[0:np_, :].unsqueeze(1).to_broadcast([np_, NH, 128]), op=ALU.mult),
             r=[tbk, "gl"], w=[tbk])
        yield
        if mid_hook is not None:
            mid_hook()
        if OPT['YIELD_QT']:
            yield
        tpt, tpk = tp.next()
        for h in range(NH):
            pe(lambda e, h=h: e.transpose(tpt[:, h * np_:(h + 1) * np_], q_[0:np_, h * 128:(h + 1) * 128],
                                          ident[0:np_, 0:np_]), r=[qk, "ident"], w=[tpk])
        QT_, QTk = QT.next()
        evac(QT_[:, :, 0:np_], tpt[:, 0:NH * np_].rearrange("p (h t) -> p h t", t=np_), [tpk], [QTk],
             scale_ap=gqc[:, 0:1], scale_key="gqc")
        yield
        tpt, tpk = tp.next()
        for h in range(NH):
            pe(lambda e, h=h: e.transpose(tpt[:, h * np_:(h + 1) * np_], kb_[0:np_, h * 128:(h + 1) * 128],
                                          ident[0:np_, 0:np_]), r=[kbk, "ident"], w=[tpk])
        kt_ap, kt_keys = kt_dst
        evac(kt_ap, tpt[:, 0:NH * np_].rearrange("p (h t) -> p h t", t=np_), [tpk], kt_keys,
             scale_ap=gkc[:, 0:1], scale_key="gkc")
        yield
        sg_, sgk = zp.next()
        for h in range(NH):
            pe(lambda e, h=h: e.matmul(sg_[0:np_, h * 128:(h + 1) * 128], lhsT=WT[0:np_, h, 0:np_],
                                       rhs=vn_[0:np_, h * 128:(h + 1) * 128], start=True, stop=True),
               r=[vnk, "WT"], w=[sgk])
        y_, yk = yb.next()
        for h in range(NH):
            dve(lambda e, h=h: e.scalar_tensor_tensor(out=y_[0:np_, h * 128:(h + 1) * 128],
                                                      in0=sg_[0:np_, h * 128:(h + 1) * 128], scalar=bS[0:np_, h:h + 1],
                                                      in1=ta_[0:np_, h * 128:(h + 1) * 128], op0=ALU.add, op1=ALU.mult),
                r=[sgk, "bS", tak], w=[(yk, "a")])
        state = dict(QT=QT_, QTk=QTk, yb=y_, ybk=yk, tb=tb_, tbk=tbk)
        yield state

    def finalize_head(O_, Ok, o_, ok, h, np_, nsum=1):
        r_, rk = r2.next()
        if nsum == 1:
            dve(lambda e: e.reciprocal(out=r_[0:np_, 0:2],
                                       in_=O_[0:np_, :].rearrange("p (c n) -> p c n", c=2)[:, :, 128]), r=[Ok], w=[rk])
        else:
            dve(lambda e: e.tensor_reduce(out=r_[0:np_, 0:2],
                                          in_=O_[0:np_, :].rearrange("p (c n) -> p c n", c=2)[:, :, 128:128 + nsum],
                                          axis=AX.X, op=ALU.add), r=[Ok], w=[rk])
            dve(lambda e: e.reciprocal(out=r_[0:np_, 0:2], in_=r_[0:np_, 0:2]), r=[rk], w=[rk])
        t_, tk = t1.next()
        dve(lambda e: e.tensor_scalar(out=t_[0:np_, :], in0=O_[0:np_, 256:384], scalar1=r_[0:np_, 1:2],
                                      scalar2=nlam[0:np_, 0:1], op0=ALU.mult, op1=ALU.mult),
            r=[Ok, rk, "nlam"], w=[tk])
        dve(lambda e: e.scalar_tensor_tensor(out=o_[0:np_, h * 128:(h + 1) * 128], in0=O_[0:np_, 0:128],
                                             scalar=r_[0:np_, 0:1], in1=t_[0:np_, :], op0=ALU.mult, op1=ALU.add),
            r=[Ok, rk, tk], w=[(ok, h)])

    def stage_c(l, np_, st, o_, ok, X, Xk, x_store):
        y_, yk, tb_, tbk = st["yb"], st["ybk"], st["tb"], st["tbk"]
        okeys = [(ok, h) for h in range(NH)]
        s_, sk = sq.next()
        act(lambda e: e.activation(out=s_[0:np_, :], in_=o_[0:np_, :], func=AF.Square), r=okeys, w=[sk])
        st_, stk = stat.next()
        dve(lambda e: e.tensor_reduce(out=st_[0:np_, 0:NH], in_=s_[0:np_, :].rearrange("p (a b) -> p a b", b=128),
                                      axis=AX.X, op=ALU.add), r=[sk], w=[stk])
        rs_ap, rsk = rsqrt_cols(st_[0:np_, 0:NH], NH, np_, 1.0 / 128, [stk])
        if OPT['LAG']:
            yield
        ceng = pool if OPT['C_POOL'] else dve
        ceng(lambda e: e.tensor_tensor(out=o_[0:np_, :].rearrange("p (a b) -> p a b", b=128),
                                       in0=o_[0:np_, :].rearrange("p (a b) -> p a b", b=128),
                                       in1=rs_ap.unsqueeze(2).to_broadcast([np_, NH, 128]), op=ALU.mult),
             r=okeys + [rsk], w=okeys)
        ceng(lambda e: e.tensor_tensor(out=y_[0:np_, 512:1024], in0=o_[0:np_, :], in1=tb_[0:np_, :], op=ALU.mult),
             r=okeys + [tbk], w=[(yk, "b")])
        yield
        if OPT['YIELD_C0']:
            yield
        tpt, tpk = tp.next()
        for k in range(8):
            pe(lambda e, k=k: e.transpose(tpt[:, k * np_:(k + 1) * np_], y_[0:np_, k * 128:(k + 1) * 128],
                                          ident[0:np_, 0:np_]), r=[(yk, "a"), (yk, "b"), "ident"], w=[tpk])
        yT_, yTk = yT.next()
        evac(yT_[:, :, 0:np_], tpt[:, 0:8 * np_].rearrange("p (k t) -> p k t", t=np_), [tpk], [yTk])
        yield
        if OPT['YIELD_YT']:
            yield
        xo_, xok = xo.next()
        for n in range(2):
            z_, zk = zp.next()
            for k in range(8):
                pe(lambda e, k=k, n=n: e.matmul(z_[0:np_, :], lhsT=yT_[:, k, 0:np_],
                                                rhs=wout[:, k, n * 512:(n + 1) * 512], start=(k == 0), stop=(k == 7)),
                   r=[yTk, ("wout", n)], w=[zk])
            dve(lambda e, n=n: e.tensor_tensor(out=xo_[0:np_, n * 512:(n + 1) * 512], in0=z_[0:np_, :],
                                               in1=X[:, n * 512:(n + 1) * 512], op=ALU.add),
                r=[zk, Xk], w=[xok])
            yield
        x_store(xo_[0:np_, :], xok)
        yield

    def prompt_n(l, t, box):
        X_, Xk = xt.next()
        src_ = xp if l == 0 else xmid
        P.dma("sp", X_[:], src_[t * 128:(t + 1) * 128, :], reads=[("xmid", t)] if l > 0 else [], writes=[Xk])
        h_, hk, fin = stage_n(128, X_[:], Xk)
        box.update(X=X_[:], Xk=Xk, h=h_, hk=hk)
        yield
        if fin is not None:
            fin()
            yield
        if OPT['TH_IN_N']:
            box["hT"] = stage_th(128, h_, hk)
            yield

    def prompt_a(l, t, box, after_proj=None, mid_hook=None):
        kt_dst = (KT[:, :, t * 128:(t + 1) * 128], [("KT", t)])
        va_dst = (VA[:, t, :, 0:128], [("VA", t)], (OPT['PAIR'] and t % 2 == 1))
        g = stage_a(l, 128, box["h"], box["hk"], kt_dst, va_dst,
                    nkp[l, t * 128:(t + 1) * 128, :], nvp[l, t * 128:(t + 1) * 128, :], after_proj=after_proj,
                    mid_hook=mid_hook, hT_pre=box.get("hT"))
        late = (t - 1) >= OPT['COPY_DVE_T']
        early = (t - 1) < OPT['EVAC_ACT_T']
        while True:
            copy_on_dve[0] = late
            evac_on_act[0] = early
            try:
                r = next(g)
            except StopIteration:
                break
            finally:
                copy_on_dve[0] = False
                evac_on_act[0] = False
            if isinstance(r, dict):
                box["st"] = r
            yield

    def prompt_b(l, t, box):
        st = box["st"]
        QT_, QTk = st["QT"], st["QTk"]
        o_, ok = ob.next()
        groups = []
        for h in range(NH):
            j = 0
            while j <= t:
                if OPT['PAIR'] and j % 2 == 0 and j + 1 <= t:
                    groups.append([(h, j), (h, j + 1)])
                    j += 2
                else:
                    groups.append([(h, j)])
                    j += 1
        units = []
        cur, n = [], 0
        for g in groups:
            if n + len(g) > 4:
                units.append(cur)
                cur, n = [], 0
            cur.append(g)
            n += len(g)
        if cur:
            units.append(cur)
        Obuf = {}
        defer = []

        def flush():
            while defer:
                O_, Ok, h = defer.pop(0)
                finalize_head(O_, Ok, o_, ok, h, 128, nsum=3)

        def qk_unit(unit):
            S_, Sk = su.next()
            s = 0
            for g in unit:
                for (h, j) in g:
                    last = (j != t)
                    pe(lambda e, s=s, h=h, j=j, last=last: e.matmul(
                        S_[:, 0, s * 128:(s + 1) * 128], lhsT=KT[0:64, h, j * 128:(j + 1) * 128],
                        rhs=QT_[0:64, h, :], start=True, stop=last), r=[("KT", j), QTk], w=[Sk])
                    pe(lambda e, s=s, h=h, j=j, last=last: e.matmul(
                        S_[:, 1, s * 128:(s + 1) * 128], lhsT=KT[64:128, h, j * 128:(j + 1) * 128],
                        rhs=QT_[64:128, h, :], start=True, stop=last), r=[("KT", j), QTk], w=[Sk])
                    if j == t:
                        pe(lambda e, s=s, h=h: e.matmul(S_[:, 0, s * 128:(s + 1) * 128], lhsT=ident[:],
                                                        rhs=Dt[:, h, 0:128], start=False, stop=True),
                           r=["ident", "Dt"], w=[Sk])
                        pe(lambda e, s=s, h=h: e.matmul(S_[:, 1, s * 128:(s + 1) * 128], lhsT=ident[:],
                                                        rhs=Dt[:, h, 0:128], start=False, stop=True),
                           r=["ident", "Dt"], w=[Sk])
                    s += 1
            return S_, Sk

        def exp_unit(unit, S_, Sk):
            pts = []
            s = 0
            for g in unit:
                h, j0 = g[0]
                w_ = 128 * len(g)
                p_, pk = PT.next()
                act(lambda e, s=s, h=h, j0=j0, p_=p_, w_=w_: e.activation(
                    out=p_[:, :, 0:w_], in_=S_[:, :, s * 128:s * 128 + w_], func=AF.Exp,
                    bias=btab[:, h, (t - j0):(t - j0) + 1]), r=[Sk, "btab"], w=[pk])
                pts.append((p_, pk))
                s += len(g)
            return pts

        def av_unit(unit, pts):
            for g, (p_, pk) in zip(unit, pts):
                for i, (h, j) in enumerate(g):
                    if j == 0:
                        if len(defer) >= 2:
                            flush()
                        Obuf[h] = opb.next()
                    O_, Ok = Obuf[h]
                    pe(lambda e, h=h, j=j, i=i, p_=p_, O_=O_: e.matmul(
                        O_[:, 0:131], lhsT=p_[:, 0, i * 128:(i + 1) * 128], rhs=VA[:, j, h, 0:131],
                        start=(j == 0), stop=(j == t)), r=[pk, ("VA", j)], w=[Ok])
                    pe(lambda e, h=h, j=j, i=i, p_=p_, O_=O_: e.matmul(
                        O_[:, 256:387], lhsT=p_[:, 1, i * 128:(i + 1) * 128], rhs=VA[:, j, h, 0:131],
                        start=False, stop=(j == t), skip_group_check=True), r=[pk, ("VA", j)], w=[Ok])
                    if j == t:
                        if OPT['LAG']:
                            defer.append((O_, Ok, h))
                        else:
                            finalize_head(O_, Ok, o_, ok, h, 128, nsum=3)

        pend = []
        for unit in units:
            flush()
            S_, Sk = qk_unit(unit)
            pend.append((unit, exp_unit(unit, S_, Sk)))
            if len(pend) > 2:
                av_unit(*pend.pop(0))
            yield
        while pend:
            flush()
            av_unit(*pend.pop(0))
            yield
        flush()

        box.update(o=o_, ok=ok)

    def prompt_c(l, t, box):
        def x_store(xo_ap, xok):
            if l == L - 1:
                P.dma("sp", yp[t * 128:(t + 1) * 128, :], xo_ap, reads=[xok], final=True)
            else:
                P.dma("sp", xmid[t * 128:(t + 1) * 128, :], xo_ap, reads=[xok], writes=[("xmid", t)])

        for _ in stage_c(l, 128, box["st"], box["o"], box["ok"], box["X"], box["Xk"], x_store):
            yield

    def cache_load(l, b, h, box):
        kc_, kck = Kc.next()
        vs_, vsk = Vs.next()
        P.dma("pool", kc_[:], ck[l, b].rearrange("(j p) n -> p j n", p=128)[:, :, h * 128:(h + 1) * 128],
              writes=[kck])
        P.dma("pool", vs_[:, :, 0:128], cv[l, b].rearrange("(j p) n -> p j n", p=128)[:, :, h * 128:(h + 1) * 128],
              writes=[vsk])
        box[("cache", h)] = (kc_, kck, vs_, vsk)

    def sample_n(l, b, box):
        np_ = DEC
        Xt_, Xk = xt.next()
        X = Xt_[0:np_, :]
        if l == 0:
            P.dma("sp", X, xs[b], writes=[Xk])
        else:
            P.dma("sp", X, xsmid[b], reads=[("xsmid", b)], writes=[Xk])
        h_, hk, fin = stage_n(np_, X, Xk)
        box.update(X=X, Xk=Xk, h=h_, hk=hk)
        yield
        if fin is not None:
            fin()
            yield
        if OPT['TH_IN_N']:
            box["hT"] = stage_th(np_, h_, hk)
            yield

    def sample_a(l, b, box, after_proj=None):
        np_ = DEC
        kt_, ktk = KTn.next()
        vn_, vnk = Vn.next()
        box.update(kt=kt_, ktk=ktk, vn=vn_, vnk=vnk)
        g = stage_a(l, np_, box["h"], box["hk"], (kt_[:, :, :], [ktk]), (vn_[:, :, 0:128], [vnk]), nks[l, b], nvs[l, b],
                    sgu_out=nsg[l, b], after_proj=after_proj, hT_pre=box.get("hT"))
        for i, r in enumerate(g):
            if isinstance(r, dict):
                box["st"] = r
            yield

    def sample_b(l, b, box):
        np_ = DEC
        st = box["st"]
        kt_, ktk, vn_, vnk = box["kt"], box["ktk"], box["vn"], box["vnk"]
        QT_, QTk = st["QT"], st["QTk"]
        o_, ok = ob.next()
        cache_load(l, b, 0, box)
        cache_load(l, b, 1, box)
        for h in range(NH):
            kc_, kck, vs_, vsk = box[("cache", h)]
            tpt, tpk = tp.next()
            for j in range(8):
                pe(lambda e, j=j: e.transpose(tpt[:, j * 128:(j + 1) * 128], kc_[:, j, :], ident[:]),
                   r=[kck, "ident"], w=[tpk])
            kts_, ktsk = KTs.next()
            evac(kts_[:], tpt[:], [tpk], [ktsk])
            yield
            sps, Sk_ = su.next()
            skeys = [Sk_]
            for j in range(8):
                for c in range(2):
                    pe(lambda e, j=j, c=c: e.matmul(sps[:, c, j * DEC:(j + 1) * DEC],
                                                    lhsT=kts_[c * 64:(c + 1) * 64, j * 128:(j + 1) * 128],
                                                    rhs=QT_[c * 64:(c + 1) * 64, h, 0:DEC], start=True, stop=True),
                       r=[ktsk, QTk], w=skeys)
            for c in range(2):
                pe(lambda e, c=c: e.matmul(sps[0:DEC, c, 128:128 + DEC], lhsT=kt_[c * 64:(c + 1) * 64, h, 0:DEC],
                                           rhs=QT_[c * 64:(c + 1) * 64, h, 0:DEC], start=True, stop=False),
                   r=[ktk, QTk], w=skeys)
            for c in range(2):
                pe(lambda e, c=c: e.matmul(sps[0:DEC, c, 128:128 + DEC], lhsT=ident[:, 0:DEC],
                                           rhs=Dt[:, h, 0:DEC], start=False, stop=True),
                   r=["ident", "Dt"], w=skeys)
            sb_, sbk = ssb.next()
            dve(lambda e: e.tensor_tensor(out=sb_[:], in0=sps[:, :, 0:128],
                                          in1=bsx0[:, h, :, :].rearrange("p j q -> p (j q)").unsqueeze(1).to_broadcast([128, 2, 128]),
                                          op=ALU.add), r=skeys + ["bsx0"], w=[sbk])
            p_, pk = PTs.next()
            act(lambda e: e.activation(out=p_[:], in_=sb_[:], func=AF.Exp, bias=negM[:, 0:1]), r=[sbk, "negM"], w=[pk])
            pn_, pnk = PTn.next()
            act(lambda e: e.activation(out=pn_[:], in_=sps[0:DEC, :, 128:128 + DEC], func=AF.Exp,
                                       bias=btab[0:DEC, h, 0:1]), r=skeys + ["btab"], w=[pnk])
            yield
            O_, Ok = opb.next()
            for c in range(2):
                for j in range(8):
                    pe(lambda e, j=j, c=c: e.matmul(O_[0:DEC, c * 256:c * 256 + 129], lhsT=p_[:, c, j * DEC:(j + 1) * DEC],
                                                    rhs=vs_[:, j, 0:129], start=(c == 0 and j == 0), stop=False,
                                                    skip_group_check=True), r=[pk, vsk], w=[Ok])
                pe(lambda e, c=c: e.matmul(O_[0:DEC, c * 256:c * 256 + 129], lhsT=pn_[:, c, :], rhs=vn_[:, h, 0:129],
                                           start=False, stop=True, skip_group_check=True), r=[pnk, vnk], w=[Ok])
            finalize_head(O_, Ok, o_, ok, h, DEC)
            if h + 2 < NH:
                cache_load(l, b, h + 2, box)
            yield

        box.update(o=o_, ok=ok)

    def sample_c(l, b, box):
        def x_store(xo_ap, xok):
            if l == L - 1:
                P.dma("sp", ys[b], xo_ap, reads=[xok], final=True)
            else:
                P.dma("sp", xsmid[b], xo_ap, reads=[xok], writes=[("xsmid", b)])

        for _ in stage_c(l, DEC, box["st"], box["o"], box["ok"], box["X"], box["Xk"], x_store):
            yield

    def run(g):
        for _ in g:
            pass

    def zipgen(gens):
        gl_ = [g for g, n in gens]
        perm = {0: None, 1: (1, 0, 2, 3), 2: (2, 0, 1, 3), 3: (3, 0, 1, 2), 4: (0, 2, 1, 3), 5: (3, 1, 0, 2)}[OPT['ORDER']]
        if perm is not None and len(gl_) == 4:
            gl_ = [gl_[p] for p in perm]
        while gl_:
            for g in list(gl_):
                try:
                    next(g)
                except StopIteration:
                    gl_.remove(g)

    def load_w_in(l):
        src_ = w_in[l].rearrange("(kt p) n -> p kt n", p=128)
        for c in CHUNK_ORDER:
            P.dma("pool", win[:, :, c * 512:(c + 1) * 512], src_[:, :, c * 512:(c + 1) * 512],
                  writes=[("win", c)])

    def load_w_out(l):
        src2 = w_out[l].rearrange("(kt p) n -> p kt n", p=128)
        for n in range(2):
            P.dma("pool", wout[:, :, n * 512:(n + 1) * 512], src2[:, :, n * 512:(n + 1) * 512],
                  writes=[("wout", n)])

    tiles = []
    for l in range(L):
        for t in range(NT):
            tiles.append((l, "p", t))
        for b in range(NS):
            tiles.append((l, "s", b))
    boxes = [dict() for _ in tiles]

    def gen_n(i):
        l, kind, x = tiles[i]
        return prompt_n(l, x, boxes[i]) if kind == "p" else sample_n(l, x, boxes[i])

    def gen_a(i):
        l, kind, x = tiles[i]
        hook = None
        if i + 1 < len(tiles) and tiles[i + 1][0] != l:
            src_ = w_in[l + 1].rearrange("(kt p) n -> p kt n", p=128)

            def hook(c):
                P.dma("pool", win[:, :, c * 512:(c + 1) * 512], src_[:, :, c * 512:(c + 1) * 512],
                      writes=[("win", c)])
        mid = None
        if i == 0 or tiles[i - 1][0] != l:
            mid = (lambda l=l: load_params_a2(l))
        return prompt_a(l, x, boxes[i], hook, mid) if kind == "p" else sample_a(l, x, boxes[i], hook)

    def gen_b(i):
        l, kind, x = tiles[i]
        return prompt_b(l, x, boxes[i]) if kind == "p" else sample_b(l, x, boxes[i])

    def gen_c(i):
        l, kind, x = tiles[i]
        return prompt_c(l, x, boxes[i]) if kind == "p" else sample_c(l, x, boxes[i])

    def layer_of(i):
        return tiles[i][0] if 0 <= i < len(tiles) else None

    NTT = len(tiles)
    load_params_n(0)
    setup_min()
    run(gen_n(0))
    if NTT > 1:
        run(gen_n(1))
    load_w_in(0)
    setup_consts()
    load_params_a(0)
    load_params_b(0)
    load_w_out(0)
    run(gen_a(0))
    for i in range(NTT + 1):
        gens = []
        if i < NTT:
            l_, kind_, x_ = tiles[i]
            nb = (x_ + 1) + 1 if kind_ == "p" else 3 * NH
            gens.append((gen_b(i), nb))
        if i + 1 < NTT:
            if layer_of(i + 1) != layer_of(i):
                load_params_a(layer_of(i + 1))
            gens.append((gen_a(i + 1), 11))
        if i >= 1:
            gens.append((gen_c(i - 1), 5))
        if i + 2 < NTT:
            if layer_of(i + 2) != layer_of(i + 1):
                load_params_n(layer_of(i + 2))
            gens.append((gen_n(i + 2), 1))
        zipgen(gens)
        if 1 <= i < NTT and layer_of(i) != layer_of(i - 1):
            load_w_out(layer_of(i))
        if i + 1 < NTT and layer_of(i + 1) != layer_of(i):
            load_params_b(layer_of(i + 1))

    sems = {e: es.enter_context(nc.semaphore("s_" + e)) for e in ENGS}
    dsems = {e: [es.enter_context(nc.semaphore(f"d_{e}{i}")) for i in range(NDMA_SEM)] for e in ("sp", "pool")}
    for e in ENGS:
        dsems.setdefault(e, [])
    with nc.Block() as block:
        @block.tensor
        def _(e):
            P.emit_engine("pe", e, sems, dsems)

        @block.scalar
        def _(e):
            P.emit_engine("act", e, sems, dsems)

        @block.vector
        def _(e):
            P.emit_engine("dve", e, sems, dsems)

        @block.gpsimd
        def _(e):
            P.emit_engine("pool", e, sems, dsems)

        @block.sync
        def _(e):
            P.emit_engine("sp", e, sems, dsems)
    es.close()
    return nc, P


_CACHE = {}


def _get_program(T, L, NS):
    key = (T, L, NS)
    if key not in _CACHE:
        _CACHE[key] = build_program(T, L, NS)[0]
    return _CACHE[key]


def kernel(x_prompt, x_sample, cache_k, cache_v, norm_g, w_in, sgu_norm_g, sgu_w, sgu_b,
           q_norm_g, k_norm_g, lambda_q1, lambda_k1, lambda_q2, lambda_k2, subln_g, w_out):
    f = lambda a: np.ascontiguousarray(np.asarray(a, dtype=np.float32))
    x_prompt, x_sample, cache_k, cache_v = f(x_prompt), f(x_sample), f(cache_k), f(cache_v)
    B, T, _ = x_prompt.shape
    L = w_in.shape[0]
    n_cores = B
    NS = x_sample.shape[0] // n_cores
    nc = _get_program(T, L, NS)
    shared = {
        "norm_g": f(norm_g), "w_in": f(w_in), "sgu_norm_g": f(sgu_norm_g).reshape(L, 512), "sgu_w": f(sgu_w),
        "sgu_b": f(sgu_b), "q_norm_g": f(q_norm_g), "k_norm_g": f(k_norm_g), "lq1": f(lambda_q1),
        "lk1": f(lambda_k1), "lq2": f(lambda_q2), "lk2": f(lambda_k2), "subln_g": f(subln_g), "w_out": f(w_out),
    }
    ckr = cache_k.reshape(L, B * NS, PAST, 512)
    cvr = cache_v.reshape(L, B * NS, PAST, 512)
    in_maps = []
    for c in range(n_cores):
        m = dict(shared)
        m["xp"] = x_prompt[c]
        m["xs"] = x_sample[c * NS:(c + 1) * NS]
        m["ck"] = np.ascontiguousarray(ckr[:, c * NS:(c + 1) * NS])
        m["cv"] = np.ascontiguousarray(cvr[:, c * NS:(c + 1) * NS])
        in_maps.append(m)
    res = run_bass_kernel_spmd(nc, in_maps, core_ids=list(range(n_cores)))
    rs = res.results
    y_prompt = np.stack([r["yp"] for r in rs], axis=0)
    y_sample = np.concatenate([r["ys"] for r in rs], axis=0)
    nk_p = np.stack([r["nkp"] for r in rs], axis=1).reshape(L, B, T, NH, 2, 64)
    nv_p = np.stack([r["nvp"] for r in rs], axis=1).reshape(L, B, T, NH, 128)
    nk_s = np.concatenate([r["nks"] for r in rs], axis=1).reshape(L, B * NS, DEC, NH, 2, 64)
    nv_s = np.concatenate([r["nvs"] for r in rs], axis=1).reshape(L, B * NS, DEC, NH, 128)
    ns_g = np.concatenate([r["nsg"] for r in rs], axis=1).reshape(L, B * NS, DEC, 512)
    return (y_prompt.astype(np.float32), y_sample.astype(np.float32), nk_p.astype(np.float32),
            nv_p.astype(np.float32), nk_s.astype(np.float32), nv_s.astype(np.float32), ns_g.astype(np.float32))
```

```python
import math
from contextlib import ExitStack

import numpy as np
import concourse.bass as bass
import concourse.mybir as mybir
from concourse.bass_utils import run_bass_kernel_spmd

F32 = mybir.dt.float32
BF16 = mybir.dt.bfloat16
AF = mybir.ActivationFunctionType
ALU = mybir.AluOpType
AX = mybir.AxisListType

ENGS = ("pe", "act", "dve", "pool", "sp")
_SBUF_FREE = [0]
OPT = dict((('TH_IN_N', 0), ('YIELD_YT', 1), ('YIELD_C0', 1), ('YIELD_QT', 0), ('LAG', 1), ('PAIR', 1), ('YIELD_TH', 1), ('C_POOL', 0), ('VA_POOL', 0), ('ORDER', 0), ('COPY_DVE_T', 99), ('EVAC_ACT_T', 0)))
NDMA_SEM = 8

D_MODEL = 1024
IN_W = 3584
NH = 4
EPS = 1e-6
PAST = 1024
DEC = 16
SLOPES = [2.0 ** (-8.0 * (h + 1) / NH) for h in range(NH)]
NEG = -30000.0
C_U, C_VA, C_GA, C_Q, C_K, C_V, C_GB = range(7)


class Op:
    __slots__ = ("eng", "fn", "dma", "deps", "sig", "signum", "qidx")

    def __init__(self, eng, fn, dma):
        self.eng = eng
        self.fn = fn
        self.dma = dma
        self.deps = set()
        self.sig = False
        self.signum = None
        self.qidx = None


class _Rec:
    def __init__(self):
        self.call = None

    def __getattr__(self, name):
        def f(*a, **k):
            self.call = (name, a, k)
            return self
        return f


class Prog:
    def __init__(self):
        self.ops = {e: [] for e in ENGS}
        self.res = {}
        self.dma_ops = {e: [] for e in ENGS}
        self.final_waits = []
        self.prepared = False
        self.nwaits = 0

    def _st(self, key):
        st = self.res.get(key)
        if st is None:
            st = [None, []]
            self.res[key] = st
        return st

    def op(self, eng, fn, reads=(), writes=(), dma=False):
        rec = _Rec()
        fn(rec)
        assert rec.call is not None
        o = Op(eng, rec.call, dma)
        deps = set()
        for r in reads:
            st = self._st(r)
            if st[0] is not None:
                deps.add((st[0], 0))
        for w in writes:
            st = self._st(w)
            if st[0] is not None:
                deps.add((st[0], 1))
            for rd in st[1]:
                deps.add((rd, 2))
        for key in tuple(reads) + tuple(writes):
            if isinstance(key, tuple) and key and key[0] == "P":
                st = self._st(key)
                for rd in st[1]:
                    if rd.eng != eng:
                        deps.add((rd, 3))
        for d, kind in deps:
            if d is o:
                continue
            if (not d.dma) and (not dma) and d.eng == eng and eng == "pe":
                continue
            o.deps.add(d)
        if dma:
            q = self.dma_ops[eng]
            o.qidx = len(q)
            if o.qidx >= NDMA_SEM:
                o.deps.add(q[o.qidx - NDMA_SEM])
            q.append(o)
        for r in reads:
            self._st(r)[1].append(o)
        for w in writes:
            st = self._st(w)
            st[0] = o
            st[1] = []
        self.ops[eng].append(o)
        return o

    def dma(self, eng, out, in_, reads=(), writes=(), final=False, **kw):
        def fn(e):
            return e.dma_start(out=out, in_=in_, **kw)
        o = self.op(eng, fn, reads, writes, dma=True)
        if final:
            self.final_waits.append(o)
        return o

    def prepare(self):
        for e in ENGS:
            for o in self.ops[e]:
                for d in o.deps:
                    d.sig = True
        for o in self.final_waits:
            o.sig = True
        for e in ENGS:
            n = 0
            for o in self.ops[e]:
                if o.dma:
                    continue
                if o.sig:
                    n += 1
                    o.signum = n
        self.prepared = True

    def emit_engine(self, e, eng, sems, dsems):
        if not self.prepared:
            self.prepare()

        def target(d):
            if d.dma:
                return (dsems[d.eng][d.qidx % NDMA_SEM], 16 * (d.qidx // NDMA_SEM + 1))
            return (sems[d.eng], d.signum)

        wm = {}
        for o in self.ops[e]:
            need = {}
            for d in o.deps:
                s, v = target(d)
                k = id(s)
                if wm.get(k, 0) >= v:
                    continue
                if k not in need or need[k][1] < v:
                    need[k] = (s, v)
            for k, (s, v) in need.items():
                eng.wait_ge(s, v)
                wm[k] = v
                self.nwaits += 1
            name_, a_, k_ = o.fn
            ins = getattr(eng, name_)(*a_, **k_)
            if o.dma:
                ins.then_inc(dsems[e][o.qidx % NDMA_SEM], 16)
            elif o.sig:
                ins.then_inc(sems[e], 1)
        if e == "sp":
            for o in self.final_waits:
                s, v = target(o)
                if wm.get(id(s), 0) >= v:
                    continue
                eng.wait_ge(s, v)
                wm[id(s)] = v


def _bf16_round(x):
    u = np.array([x], dtype=np.float32).view(np.uint32)
    u = ((u + np.uint32(0x7FFF) + ((u >> np.uint32(16)) & np.uint32(1))) & np.uint32(0xFFFF0000)).astype(np.uint32)
    return float(u.view(np.float32)[0])


def _bf16_split3(c):
    hi = _bf16_round(c)
    mid = _bf16_round(c - hi)
    lo = _bf16_round(c - hi - mid)
    return hi, mid, lo


class Rot:
    def __init__(self, alloc, name, n, shape, dt):
        self.bufs = [alloc(f"{name}{i}", shape, dt) for i in range(n)]
        self.name = name
        self.i = -1

    def next(self):
        self.i = (self.i + 1) % len(self.bufs)
        return self.bufs[self.i], (self.name, self.i)


def build_program(T=2048, L=2, NS=2, interleave=True):
    NT = T // 128
    nc = bass.Bass("TRN2", target_bir_lowering=False)

    def din(name, shape):
        return nc.dram_tensor(name, list(shape), F32, kind="ExternalInput").ap()

    def dout(name, shape):
        return nc.dram_tensor(name, list(shape), F32, kind="ExternalOutput").ap()

    xp = din("xp", [T, D_MODEL])
    xs = din("xs", [NS, DEC, D_MODEL])
    ck = din("ck", [L, NS, PAST, 512])
    cv = din("cv", [L, NS, PAST, 512])
    norm_g = din("norm_g", [L, D_MODEL])
    w_in = din("w_in", [L, D_MODEL, IN_W])
    sgu_norm_g = din("sgu_norm_g", [L, 512])
    sgu_w = din("sgu_w", [L, NH, 128, 128])
    sgu_b = din("sgu_b", [L, NH, 128])
    q_norm_g = din("q_norm_g", [L, 64])
    k_norm_g = din("k_norm_g", [L, 64])
    lq1 = din("lq1", [L, 64])
    lk1 = din("lk1", [L, 64])
    lq2 = din("lq2", [L, 64])
    lk2 = din("lk2", [L, 64])
    subln_g = din("subln_g", [L, 128])
    w_out = din("w_out", [L, D_MODEL, D_MODEL])

    yp = dout("yp", [T, D_MODEL])
    ys = dout("ys", [NS, DEC, D_MODEL])
    nkp = dout("nkp", [L, T, 512])
    nvp = dout("nvp", [L, T, 512])
    nks = dout("nks", [L, NS, DEC, 512])
    nvs = dout("nvs", [L, NS, DEC, 512])
    nsg = dout("nsg", [L, NS, DEC, 512])
    xmid = nc.dram_tensor("xmid", [T, D_MODEL], F32, kind="Internal").ap()
    xsmid = nc.dram_tensor("xsmid", [NS, DEC, D_MODEL], F32, kind="Internal").ap()

    P = Prog()
    es = ExitStack()

    def sb(name, shape, dt=F32):
        return es.enter_context(nc.sbuf_tensor(name, list(shape), dt))

    def ps(name, shape, dt=F32):
        return es.enter_context(nc.psum_tensor(name, list(shape), dt))

    win = sb("win", [128, 8, IN_W], BF16)
    wout = sb("wout", [128, 8, D_MODEL], BF16)
    KT = sb("KT", [128, NH, T], BF16)
    VA = sb("VA", [128, NT, NH, 132], BF16)
    ident = sb("ident", [128, 128], BF16)
    Dt = sb("Dt", [128, NH, 128], BF16)
    mhalf = sb("mhalf", [128, 8], F32)
    vsc = sb("vsc", [128, NH], F32)
    btab0 = sb("btab0", [128, NH, 16], F32)
    btab = sb("btab", [128, NH, 16], F32)
    bsx0 = sb("bsx0", [128, NH, 8, DEC], F32)
    gN = sb("gN", [128, D_MODEL], F32)
    gS = sb("gS", [128, 512], F32)
    gqc = sb("gqc", [128, 1], F32)
    gkc = sb("gkc", [128, 1], F32)
    gl = sb("gl", [128, 128], F32)
    WT = sb("WT", [128, NH, 128], BF16)
    bS = sb("bS", [128, NH], F32)
    nlam = sb("nlam", [128, 1], F32)
    Mb = sb("Mb", [128, 1], F32)
    small = sb("small", [128, 7, 64], F32)
    gLr = sb("gLr", [128, 128], F32)
    sc = sb("sc", [128, 16], F32)
    negM = sb("negM", [128, 1], F32)

    xt = Rot(sb, "xt", 4, [128, D_MODEL], F32)
    hb = Rot(sb, "hb", 2, [128, D_MODEL], BF16)
    hT = Rot(sb, "hT", 2, [128, 8, 128], BF16)
    sq = Rot(sb, "sq", 2, [128, 512], F32)
    zr = Rot(sb, "zr", 3, [128, 512], F32)
    stat = Rot(sb, "stat", 8, [128, 8], F32)
    rst = Rot(sb, "rst", 8, [128, 8], F32)
    ta = Rot(sb, "ta", 1, [128, 512], F32)
    van = Rot(sb, "van", 1, [128, 512], BF16)
    qn = Rot(sb, "qn", 1, [128, 512], BF16)
    knb = Rot(sb, "knb", 1, [128, 512], BF16)
    vf = Rot(sb, "vf", 1, [128, 512], F32)
    QT = Rot(sb, "QT", 2, [128, NH, 128], BF16)
    tb = Rot(sb, "tb", 3, [128, 512], F32)
    PT = Rot(sb, "PT", 6, [128, 2, 256], BF16) if OPT["PAIR"] else Rot(sb, "PT", 12, [128, 2, 128], BF16)
    ob = Rot(sb, "ob", 2, [128, 512], F32)
    t1 = Rot(sb, "t1", 2, [128, 128], F32)
    r2 = Rot(sb, "r2", 4, [128, 4], F32)
    yb = Rot(sb, "yb", 3, [128, D_MODEL], BF16)
    yT = Rot(sb, "yT", 1, [128, 8, 128], BF16)
    xo = Rot(sb, "xo", 1, [128, D_MODEL], F32)
    Kc = Rot(sb, "Kc", 2, [128, 8, 128], BF16)
    KTs = Rot(sb, "KTs", 1, [128, 8 * 128], BF16)
    Vs = Rot(sb, "Vs", 2, [128, 8, 130], BF16)
    KTn = Rot(sb, "KTn", 2, [128, NH, DEC], BF16)
    Vn = Rot(sb, "Vn", 2, [DEC, NH, 130], BF16)
    ssb = Rot(sb, "ssb", 2, [128, 2, 128], F32)
    PTs = Rot(sb, "PTs", 2, [128, 2, 128], BF16)
    PTn = Rot(sb, "PTn", 2, [DEC, 2, DEC], BF16)

    wf32 = sq.bufs[0][:].rearrange("p (h s) -> p h s", h=NH)
    wbf_t = sb("wbf_t", [128, NH * 128], BF16)
    wbf = wbf_t[:].rearrange("p (h s) -> p h s", h=NH)
    WF = ("sq", 0)
    WB = "wbf_t"
    class PRot:
        def __init__(self, tag, n, shape):
            self.bufs = [ps(f"psum_{tag}{i}", shape, F32) for i in range(n)]
            self.tag = tag
            self.i = -1

        def next(self):
            self.i = (self.i + 1) % len(self.bufs)
            return self.bufs[self.i], ("P", self.tag, self.i)

    zp = PRot("zb", 2, [128, 512])
    su = PRot("su", 2, [128, 2, 512])
    opb = PRot("ob", 2, [128, 512])

    class TP:
        def next(self):
            z_, zk = zp.next()
            return z_[:].bitcast(BF16), zk

    tp = TP()

    _SBUF_FREE[0] = nc.sbuf_bytes_remaining
    def act(fn, r=(), w=()):
        return P.op("act", fn, r, w)

    def dve(fn, r=(), w=()):
        return P.op("dve", fn, r, w)

    def pool(fn, r=(), w=()):
        return P.op("pool", fn, r, w)

    def pe(fn, r=(), w=()):
        return P.op("pe", fn, r, w)

    def setup_min():
        idf = wf32
        pool(lambda e: e.memset(mhalf[:], -0.5), w=["mhalf"])
        pool(lambda e: e.memset(idf[:, 0, :], 0.0), w=[WF])
        pool(lambda e: e.affine_select(out=idf[:, 0, :], in_=idf[:, 0, :], pattern=[[-1, 128]],
                                       compare_op=ALU.not_equal, fill=1.0, base=0, channel_multiplier=1),
             r=[WF], w=[WF])
        dve(lambda e: e.tensor_copy(ident[:], idf[:, 0, :]), r=[WF], w=["ident"])

    def setup_consts():
        idf = wf32
        pool(lambda e: e.memset(VA[:, :, :, 128:132], 0.0), w=[("VA", j) for j in range(NT)])
        pool(lambda e: e.memset(VA[:, :, :, 128:129], 1.0), w=[("VA", j) for j in range(NT)])
        for h in range(NH):
            hi, mid, lo = _bf16_split3(math.exp(128.0 * SLOPES[h]))
            cval = float(np.float32(np.float32(hi) + np.float32(mid) + np.float32(lo)))
            pool(lambda e, h=h, cval=cval: e.memset(vsc[:, h:h + 1], cval), w=["vsc"])
            for j in (range(1, NT, 2) if OPT['PAIR'] else ()):
                for ci, cv_ in enumerate((hi, mid, lo)):
                    pool(lambda e, h=h, j=j, ci=ci, cv_=cv_: e.memset(VA[:, j, h, 128 + ci:129 + ci], cv_),
                         w=[("VA", j)])
        for i in range(2):
            b_ = Vs.bufs[i]
            pool(lambda e, b_=b_: e.memset(b_[:, :, 128:130], 1.0), w=[("Vs", i)])
        for i in range(2):
            b_ = Vn.bufs[i]
            pool(lambda e, b_=b_: e.memset(b_[:, :, 128:130], 1.0), w=[("Vn", i)])
        pool(lambda e: e.iota(idf[:, 1, :], pattern=[[-1, 128]], base=0, channel_multiplier=1,
                              allow_small_or_imprecise_dtypes=True), r=["ident"], w=[WF])
        for h in range(NH):
            dve(lambda e, h=h: e.tensor_scalar(out=idf[:, 2, :], in0=idf[:, 1, :], scalar1=0.0,
                                               scalar2=-2.0 * SLOPES[h], op0=ALU.max, op1=ALU.mult),
                r=[WF], w=[WF])
            dve(lambda e: e.memset(idf[64:128, 2, 0:64], NEG), r=[WF], w=[WF])
            dve(lambda e, h=h: e.tensor_copy(Dt[:, h, 0:128], idf[:, 2, :]), r=[WF], w=["Dt"])
        pool(lambda e: e.iota(idf[:, 3, 0:16], pattern=[[-128, 16]], base=0, channel_multiplier=1,
                              allow_small_or_imprecise_dtypes=True), r=["Dt"], w=[WF])
        for h in range(NH):
            dve(lambda e, h=h: e.tensor_scalar(out=btab0[:, h, :], in0=idf[:, 3, 0:16], scalar1=SLOPES[h],
                                               scalar2=None, op0=ALU.mult), r=[WF], w=["btab0"])
        pool(lambda e: e.iota(idf[:, 3, :].rearrange("p (j q) -> p j q", q=DEC), pattern=[[128, 8], [0, DEC]],
                              base=-PAST, channel_multiplier=1, allow_small_or_imprecise_dtypes=True),
             r=["btab0"], w=[WF])
        for h in range(NH):
            dve(lambda e, h=h: e.tensor_scalar(out=bsx0[:, h, :, :].rearrange("p j q -> p (j q)"),
                                               in0=idf[:, 3, :], scalar1=SLOPES[h], scalar2=None, op0=ALU.mult),
                r=[WF], w=["bsx0"])

    CHUNK_ORDER = [C_Q, C_K, C_V, C_GA, C_VA, C_U, C_GB]

    def load_weights(l):
        src = w_in[l].rearrange("(kt p) n -> p kt n", p=128)
        for c in CHUNK_ORDER:
            P.dma("pool", win[:, :, c * 512:(c + 1) * 512], src[:, :, c * 512:(c + 1) * 512],
                  writes=[("win", c)])
        src2 = w_out[l].rearrange("(kt p) n -> p kt n", p=128)
        for n in range(2):
            P.dma("pool", wout[:, :, n * 512:(n + 1) * 512], src2[:, :, n * 512:(n + 1) * 512],
                  writes=[("wout", n)])

    def load_params_n(l):
        P.dma("sp", gN[:], norm_g[l:l + 1, :].partition_broadcast(128), writes=["gN"])

    def load_params_a(l):
        lam_init = 0.8 - 0.6 * math.exp(-0.3 * l)
        P.dma("sp", gS[:], sgu_norm_g[l:l + 1, :].partition_broadcast(128), writes=["gS"])
        for i, t_ in enumerate((q_norm_g, k_norm_g)):
            P.dma("sp", small[:, i, :], t_[l:l + 1, :].partition_broadcast(128), writes=[("small", i)])
        P.dma("sp", gLr[:], subln_g[l:l + 1, :].partition_broadcast(128), writes=["gLr"])
        for half in range(2):
            P.dma("sp", gqc[half * 64:(half + 1) * 64, :], q_norm_g[l].rearrange("(d o) -> d o", o=1), writes=["gqc"],
                  allow_slow_non_contiguous=True)
            P.dma("sp", gkc[half * 64:(half + 1) * 64, :], k_norm_g[l].rearrange("(d o) -> d o", o=1), writes=["gkc"],
                  allow_slow_non_contiguous=True)
        dve(lambda e: e.tensor_scalar(out=gqc[:], in0=gqc[:], scalar1=0.125, scalar2=None, op0=ALU.mult),
            r=["gqc"], w=["gqc"])
        dve(lambda e: e.tensor_scalar(out=gl[:], in0=gLr[:], scalar1=0.5 * (1.0 - lam_init), scalar2=None,
                                      op0=ALU.mult), r=["gLr"], w=["gl"])
        dve(lambda e: e.tensor_reduce(out=sc[:, 0:1], in_=small[:, 0, :], axis=AX.X, op=ALU.max,
                                      apply_absolute_value=True), r=[("small", 0)], w=["sc0"])
        dve(lambda e: e.tensor_reduce(out=sc[:, 1:2], in_=small[:, 1, :], axis=AX.X, op=ALU.max,
                                      apply_absolute_value=True), r=[("small", 1)], w=["sc1"])
        dve(lambda e: e.tensor_scalar(out=Mb[:], in0=sc[:, 0:1], scalar1=sc[:, 1:2], scalar2=8.0,
                                      op0=ALU.mult, op1=ALU.mult), r=["sc0", "sc1"], w=["Mb"])
    def load_params_a2(l):
        P.dma("sp", wf32[:], sgu_w[l].rearrange("h t s -> t h s"), writes=[WF])
        P.dma("sp", bS[:], sgu_b[l].rearrange("h t -> t h"), writes=["bS"], allow_slow_non_contiguous=True)
        for h in range(NH):
            pool(lambda e, h=h: e.affine_select(out=wf32[:, h, :], in_=wf32[:, h, :], pattern=[[-1, 128]],
                                                compare_op=ALU.is_ge, fill=0.0, base=0, channel_multiplier=1),
                 r=[WF], w=[WF])
        dve(lambda e: e.tensor_scalar(out=wbf[:].rearrange("p h s -> p (h s)"),
                                      in0=wf32[:].rearrange("p h s -> p (h s)"), scalar1=0.5, scalar2=None,
                                      op0=ALU.mult), r=[WF], w=[WB])
        tpt, tpk = tp.next()
        for h in range(NH):
            pe(lambda e, h=h: e.transpose(tpt[:, h * 128:(h + 1) * 128], wbf[:, h, :], ident[:]),
               r=[WB, "ident"], w=[tpk])
        dve(lambda e: e.tensor_copy(WT[:].rearrange("p h t -> p (h t)"), tpt[:, 0:512]), r=[tpk], w=["WT"])
        dve(lambda e: e.tensor_scalar(out=bS[:], in0=bS[:], scalar1=0.5, scalar2=None, op0=ALU.mult),
            r=["bS"], w=["bS"])

    def load_params_b(l):
        lam_init = 0.8 - 0.6 * math.exp(-0.3 * l)
        dve(lambda e: e.tensor_scalar(out=btab[:].rearrange("p h d -> p (h d)"),
                                      in0=btab0[:].rearrange("p h d -> p (h d)"), scalar1=Mb[:, 0:1],
                                      scalar2=None, op0=ALU.subtract), r=["Mb", "btab0"], w=["btab"])
        dve(lambda e: e.tensor_scalar(out=negM[:], in0=Mb[:], scalar1=-1.0, scalar2=None, op0=ALU.mult),
            r=["Mb"], w=["negM"])
        for i, (a_, b_) in enumerate(((lq1, lk1), (lq2, lk2))):
            sa, sb_ = 2 + 2 * i, 3 + 2 * i
            P.dma("sp", small[:, sa, :], a_[l:l + 1, :].partition_broadcast(128), writes=[("small", sa)])
            P.dma("sp", small[:, sb_, :], b_[l:l + 1, :].partition_broadcast(128), writes=[("small", sb_)])
            dve(lambda e, sa=sa, sb_=sb_: e.tensor_tensor(out=small[:, 6, :], in0=small[:, sa, :], in1=small[:, sb_, :],
                                                          op=ALU.mult),
                r=[("small", sa), ("small", sb_)], w=[("small", 6)])
            dve(lambda e, i=i: e.tensor_reduce(out=sc[:, 2 + i:3 + i], in_=small[:, 6, :], axis=AX.X, op=ALU.add),
                r=[("small", 6)], w=[f"sc{2 + i}"])
            act(lambda e, i=i: e.activation(out=sc[:, 4 + i:5 + i], in_=sc[:, 2 + i:3 + i], func=AF.Exp),
                r=[f"sc{2 + i}"], w=[f"sc{4 + i}"])
        dve(lambda e: e.tensor_scalar(out=nlam[:], in0=sc[:, 5:6], scalar1=sc[:, 4:5], scalar2=-lam_init,
                                      op0=ALU.subtract, op1=ALU.add), r=["sc4", "sc5"], w=["nlam"])

    def rsqrt_cols(src_ap, n, np_, scale, rkeys):
        st_, stk = stat.next()
        rs_, rsk = rst.next()
        pool(lambda e: e.tensor_scalar(out=st_[0:np_, 0:n], in0=src_ap, scalar1=scale, scalar2=EPS,
                                       op0=ALU.mult, op1=ALU.add), r=rkeys, w=[stk])
        pool(lambda e: e.tensor_tensor(out=rs_[0:np_, 0:n], in0=st_[0:np_, 0:n], in1=mhalf[0:np_, 0:n], op=ALU.pow),
             r=[stk, "mhalf"], w=[rsk])
        return rs_[0:np_, 0:n], rsk

    copy_on_dve = [False]
    evac_on_act = [False]

    def evac(out_ap, in_ap, rkeys, wkeys, scale_ap=None, scale_key=None):
        if evac_on_act[0]:
            if scale_ap is None:
                act(lambda e: e.activation(out=out_ap, in_=in_ap, func=AF.Copy), r=rkeys, w=wkeys)
            else:
                act(lambda e: e.activation(out=out_ap, in_=in_ap, func=AF.Copy, scale=scale_ap), r=rkeys + [scale_key], w=wkeys)
        else:
            if scale_ap is None:
                dve(lambda e: e.tensor_copy(out_ap, in_ap), r=rkeys, w=wkeys)
            else:
                dve(lambda e: e.tensor_scalar(out=out_ap, in0=in_ap, scalar1=scale_ap, scalar2=None, op0=ALU.mult),
                    r=rkeys + [scale_key], w=wkeys)

    def gn_stats(z_, zk, np_, ng, gs):
        r_, rk = zr.next()
        if copy_on_dve[0]:
            dve(lambda e: e.tensor_copy(r_[0:np_, :], z_[0:np_, :]), r=[zk], w=[rk])
        else:
            act(lambda e: e.activation(out=r_[0:np_, :], in_=z_[0:np_, :], func=AF.Copy), r=[zk], w=[rk])
        s_, sk = sq.next()
        act(lambda e: e.activation(out=s_[0:np_, :], in_=z_[0:np_, :], func=AF.Square), r=[zk], w=[sk])
        st_, stk = stat.next()
        dve(lambda e: e.tensor_reduce(out=st_[0:np_, 0:ng], in_=s_[0:np_, :].rearrange("p (a b) -> p a b", b=gs),
                                      axis=AX.X, op=ALU.add), r=[sk], w=[stk])
        rs_ap, rsk = rsqrt_cols(st_[0:np_, 0:ng], ng, np_, 1.0 / gs, [stk])
        return r_, rk, rs_ap, rsk

    def gn_apply(eng_fn, zr_, zrk, rs_ap, rsk, np_, ng, gs, out_ap, out_keys):
        eng_fn(lambda e: e.tensor_tensor(out=out_ap.rearrange("p (a b) -> p a b", b=gs),
                                         in0=zr_[0:np_, :].rearrange("p (a b) -> p a b", b=gs),
                                         in1=rs_ap.unsqueeze(2).to_broadcast([np_, ng, gs]), op=ALU.mult),
               r=[zrk, rsk], w=out_keys)

    def stage_n(np_, X, Xk):
        st_, stk = stat.next()
        h_, hk = hb.next()
        act(lambda e: e.activation(out=h_[0:np_, :], in_=X, func=AF.Square, accum_out=st_[0:np_, 0:1]),
            r=[Xk], w=[hk, stk])
        rs_ap, rsk = rsqrt_cols(st_[0:np_, 0:1], 1, np_, 1.0 / D_MODEL, [stk])

        def apply():
            dve(lambda e: e.scalar_tensor_tensor(out=h_[0:np_, :], in0=X, scalar=rs_ap, in1=gN[0:np_, :],
                                                 op0=ALU.mult, op1=ALU.mult), r=[Xk, rsk, "gN"], w=[hk])
        if OPT['LAG']:
            return h_, hk, apply
        apply()
        return h_, hk, None

    def stage_th(np_, h_, hk):
        tpt, tpk = tp.next()
        for k in range(8):
            pe(lambda e, k=k: e.transpose(tpt[:, k * np_:(k + 1) * np_], h_[0:np_, k * 128:(k + 1) * 128],
                                          ident[0:np_, 0:np_]), r=[hk, "ident"], w=[tpk])
        hT_, hTk = hT.next()
        evac(hT_[:, :, 0:np_], tpt[:, 0:8 * np_].rearrange("p (k t) -> p k t", t=np_), [tpk], [hTk])
        return hT_, hTk

    def stage_a(l, np_, h_, hk, kt_dst, va_dst, k_out, v_out, sgu_out=None, after_proj=None, mid_hook=None, hT_pre=None):
        if hT_pre is None:
            hT_, hTk = stage_th(np_, h_, hk)
            yield
            if OPT['YIELD_TH']:
                yield
        else:
            hT_, hTk = hT_pre

        def proj(c):
            z_, zk = zp.next()
            for k in range(8):
                pe(lambda e, k=k: e.matmul(z_[0:np_, :], lhsT=hT_[:, k, 0:np_], rhs=win[:, k, c * 512:(c + 1) * 512],
                                           start=(k == 0), stop=(k == 7)), r=[hTk, ("win", c)], w=[zk])
            if after_proj is not None:
                after_proj(c)
            return z_, zk

        lag = OPT['LAG']
        z_, zk = proj(C_Q)
        rq_, rqk, rsq_ap, rsqk = gn_stats(z_, zk, np_, 8, 64)
        q_, qk = qn.next()

        def apply_q():
            gn_apply(dve, rq_, rqk, rsq_ap, rsqk, np_, 8, 64, q_[0:np_, :], [qk])
        if not lag:
            apply_q()
        yield
        if lag:
            apply_q()
        z_, zk = proj(C_K)
        rk_, rkk, rsk_ap, rskk = gn_stats(z_, zk, np_, 8, 64)
        kb_, kbk = knb.next()

        def apply_k():
            gn_apply(dve, rk_, rkk, rsk_ap, rskk, np_, 8, 64, kb_[0:np_, :], [kbk])
            gn_apply(pool, rk_, rkk, rsk_ap, rskk, np_, 8, 64, rk_[0:np_, :], [rkk])
            pool(lambda e: e.tensor_tensor(out=rk_[0:np_, :].rearrange("p (a b) -> p a b", b=64),
                                           in0=rk_[0:np_, :].rearrange("p (a b) -> p a b", b=64),
                                           in1=small[0:np_, 1, :].unsqueeze(1).to_broadcast([np_, 8, 64]), op=ALU.mult),
                 r=[rkk, ("small", 1)], w=[rkk])
            P.dma("sp", k_out, rk_[0:np_, :], reads=[rkk], final=True)
        if not lag:
            apply_k()
        yield
        if lag:
            apply_k()
        z_, zk = proj(C_V)
        v_, vk = vf.next()
        if copy_on_dve[0]:
            dve(lambda e: e.tensor_copy(v_[0:np_, :], z_[0:np_, :]), r=[zk], w=[vk])
        else:
            act(lambda e: e.activation(out=v_[0:np_, :], in_=z_[0:np_, :], func=AF.Copy), r=[zk], w=[vk])
        P.dma("sp", v_out, v_[0:np_, :], reads=[vk], final=True)
        va_ap, va_keys = va_dst[0], va_dst[1]
        if len(va_dst) > 2 and va_dst[2]:
            dve(lambda e: e.tensor_tensor(out=va_ap, in0=v_[0:np_, :].rearrange("p (h d) -> p h d", d=128),
                                          in1=vsc[0:np_, :].unsqueeze(2).to_broadcast([np_, NH, 128]), op=ALU.mult),
                r=[vk, "vsc"], w=va_keys)
        else:
            dve(lambda e: e.tensor_copy(va_ap, v_[0:np_, :].rearrange("p (h d) -> p h d", d=128)), r=[vk], w=va_keys)
        yield
        z_, zk = proj(C_GA)
        ta_, tak = ta.next()
        act(lambda e: e.activation(out=ta_[0:np_, :], in_=z_[0:np_, :], func=AF.Tanh, scale=0.5), r=[zk], w=[tak])
        dve(lambda e: e.scalar_tensor_tensor(out=ta_[0:np_, :], in0=ta_[0:np_, :], scalar=1.0, in1=z_[0:np_, :],
                                             op0=ALU.add, op1=ALU.mult), r=[tak, zk], w=[tak])
        yield
        z_, zk = proj(C_VA)
        rv_, rvk, rsv_ap, rsvk = gn_stats(z_, zk, np_, 4, 128)
        if sgu_out is not None:
            P.dma("sp", sgu_out, rv_[0:np_, :], reads=[rvk], final=True)
        vn_, vnk = van.next()

        def apply_va():
            gn_apply(pool if OPT['VA_POOL'] else dve, rv_, rvk, rsv_ap, rsvk, np_, 4, 128, rv_[0:np_, :], [rvk])
            pool(lambda e: e.tensor_tensor(out=vn_[0:np_, :], in0=rv_[0:np_, :], in1=gS[0:np_, :], op=ALU.mult),
                 r=[rvk, "gS"], w=[vnk])
        if not lag:
            apply_va()
        yield
        if lag:
            apply_va()
        z_, zk = proj(C_U)
        dve(lambda e: e.tensor_tensor(out=ta_[0:np_, :], in0=z_[0:np_, :], in1=ta_[0:np_, :], op=ALU.mult),
            r=[zk, tak], w=[tak])
        yield
        z_, zk = proj(C_GB)
        tb_, tbk = tb.next()
        act(lambda e: e.activation(out=tb_[0:np_, :], in_=z_[0:np_, :], func=AF.Tanh, scale=0.5), r=[zk], w=[tbk])
        dve(lambda e: e.scalar_tensor_tensor(out=tb_[0:np_, :], in0=tb_[0:np_, :], scalar=1.0, in1=z_[0:np_, :],
                                             op0=ALU.add, op1=ALU.mult), r=[tbk, zk], w=[tbk])
        pool(lambda e: e.tensor_tensor(out=tb_[0:np_, :].rearrange("p (a b) -> p a b", b=128),
                                       in0=tb_[0:np_, :].rearrange("p (a b) -> p a b", b=128),
                                       in1=gl[0:np_, :].unsqueeze(1).to_broadcast([np_, NH, 128]), op=ALU.mult),
             r=[tbk, "gl"], w=[tbk])
        yield
        if mid_hook is not None:
            mid_hook()
        if OPT['YIELD_QT']:
            yield
        tpt, tpk = tp.next()
        for h in range(NH):
            pe(lambda e, h=h: e.transpose(tpt[:, h * np_:(h + 1) * np_], q_[0:np_, h * 128:(h + 1) * 128],
                                          ident[0:np_, 0:np_]), r=[qk, "ident"], w=[tpk])
        QT_, QTk = QT.next()
        evac(QT_[:, :, 0:np_], tpt[:, 0:NH * np_].rearrange("p (h t) -> p h t", t=np_), [tpk], [QTk],
             scale_ap=gqc[:, 0:1], scale_key="gqc")
        yield
        tpt, tpk = tp.next()
        for h in range(NH):
            pe(lambda e, h=h: e.transpose(tpt[:, h * np_:(h + 1) * np_], kb_[0:np_, h * 128:(h + 1) * 128],
                                          ident[0:np_, 0:np_]), r=[kbk, "ident"], w=[tpk])
        kt_ap, kt_keys = kt_dst
        evac(kt_ap, tpt[:, 0:NH * np_].rearrange("p (h t) -> p h t", t=np_), [tpk], kt_keys,
             scale_ap=gkc[:, 0:1], scale_key="gkc")
        yield
        sg_, sgk = zp.next()
        for h in range(NH):
            pe(lambda e, h=h: e.matmul(sg_[0:np_, h * 128:(h + 1) * 128], lhsT=WT[0:np_, h, 0:np_],
                                       rhs=vn_[0:np_, h * 128:(h + 1) * 128], start=True, stop=True),
               r=[vnk, "WT"], w=[sgk])
        y_, yk = yb.next()
        for h in range(NH):
            dve(lambda e, h=h: e.scalar_tensor_tensor(out=y_[0:np_, h * 128:(h + 1) * 128],
                                                      in0=sg_[0:np_, h * 128:(h + 1) * 128], scalar=bS[0:np_, h:h + 1],
                                                      in1=ta_[0:np_, h * 128:(h + 1) * 128], op0=ALU.add, op1=ALU.mult),
                r=[sgk, "bS", tak], w=[(yk, "a")])
        state = dict(QT=QT_, QTk=QTk, yb=y_, ybk=yk, tb=tb_, tbk=tbk)
        yield state

    def finalize_head(O_, Ok, o_, ok, h, np_, nsum=1):
        r_, rk = r2.next()
        if nsum == 1:
            dve(lambda e: e.reciprocal(out=r_[0:np_, 0:2],
                                       in_=O_[0:np_, :].rearrange("p (c n) -> p c n", c=2)[:, :, 128]), r=[Ok], w=[rk])
        else:
            dve(lambda e: e.tensor_reduce(out=r_[0:np_, 0:2],
                                          in_=O_[0:np_, :].rearrange("p (c n) -> p c n", c=2)[:, :, 128:128 + nsum],
                                          axis=AX.X, op=ALU.add), r=[Ok], w=[rk])
            dve(lambda e: e.reciprocal(out=r_[0:np_, 0:2], in_=r_[0:np_, 0:2]), r=[rk], w=[rk])
        t_, tk = t1.next()
        dve(lambda e: e.tensor_scalar(out=t_[0:np_, :], in0=O_[0:np_, 256:384], scalar1=r_[0:np_, 1:2],
                                      scalar2=nlam[0:np_, 0:1], op0=ALU.mult, op1=ALU.mult),
            r=[Ok, rk, "nlam"], w=[tk])
        dve(lambda e: e.scalar_tensor_tensor(out=o_[0:np_, h * 128:(h + 1) * 128], in0=O_[0:np_, 0:128],
                                             scalar=r_[0:np_, 0:1], in1=t_[0:np_, :], op0=ALU.mult, op1=ALU.add),
            r=[Ok, rk, tk], w=[(ok, h)])

    def stage_c(l, np_, st, o_, ok, X, Xk, x_store):
        y_, yk, tb_, tbk = st["yb"], st["ybk"], st["tb"], st["tbk"]
        okeys = [(ok, h) for h in range(NH)]
        s_, sk = sq.next()
        act(lambda e: e.activation(out=s_[0:np_, :], in_=o_[0:np_, :], func=AF.Square), r=okeys, w=[sk])
        st_, stk = stat.next()
        dve(lambda e: e.tensor_reduce(out=st_[0:np_, 0:NH], in_=s_[0:np_, :].rearrange("p (a b) -> p a b", b=128),
                                      axis=AX.X, op=ALU.add), r=[sk], w=[stk])
        rs_ap, rsk = rsqrt_cols(st_[0:np_, 0:NH], NH, np_, 1.0 / 128, [stk])
        if OPT['LAG']:
            yield
        ceng = pool if OPT['C_POOL'] else dve
        ceng(lambda e: e.tensor_tensor(out=o_[0:np_, :].rearrange("p (a b) -> p a b", b=128),
                                       in0=o_[0:np_, :].rearrange("p (a b) -> p a b", b=128),
                                       in1=rs_ap.unsqueeze(2).to_broadcast([np_, NH, 128]), op=ALU.mult),
             r=okeys + [rsk], w=okeys)
        ceng(lambda e: e.tensor_tensor(out=y_[0:np_, 512:1024], in0=o_[0:np_, :], in1=tb_[0:np_, :], op=ALU.mult),
             r=okeys + [tbk], w=[(yk, "b")])
        yield
        if OPT['YIELD_C0']:
            yield
        tpt, tpk = tp.next()
        for k in range(8):
            pe(lambda e, k=k: e.transpose(tpt[:, k * np_:(k + 1) * np_], y_[0:np_, k * 128:(k + 1) * 128],
                                          ident[0:np_, 0:np_]), r=[(yk, "a"), (yk, "b"), "ident"], w=[tpk])
        yT_, yTk = yT.next()
        dve(lambda e: e.tensor_copy(yT_[:, :, 0:np_], tpt[:, 0:8 * np_].rearrange("p (k t) -> p k t", t=np_)),
            r=[tpk], w=[yTk])
        yield
        if OPT['YIELD_YT']:
            yield
        xo_, xok = xo.next()
        for n in range(2):
            z_, zk = zp.next()
            for k in range(8):
                pe(lambda e, k=k, n=n: e.matmul(z_[0:np_, :], lhsT=yT_[:, k, 0:np_],
                                                rhs=wout[:, k, n * 512:(n + 1) * 512], start=(k == 0), stop=(k == 7)),
                   r=[yTk, ("wout", n)], w=[zk])
            dve(lambda e, n=n: e.tensor_tensor(out=xo_[0:np_, n * 512:(n + 1) * 512], in0=z_[0:np_, :],
                                               in1=X[:, n * 512:(n + 1) * 512], op=ALU.add),
                r=[zk, Xk], w=[xok])
            yield
        x_store(xo_[0:np_, :], xok)
        yield

    def prompt_n(l, t, box):
        X_, Xk = xt.next()
        src_ = xp if l == 0 else xmid
        P.dma("sp", X_[:], src_[t * 128:(t + 1) * 128, :], reads=[("xmid", t)] if l > 0 else [], writes=[Xk])
        h_, hk, fin = stage_n(128, X_[:], Xk)
        box.update(X=X_[:], Xk=Xk, h=h_, hk=hk)
        yield
        if fin is not None:
            fin()
            yield
        if OPT['TH_IN_N']:
            box["hT"] = stage_th(128, h_, hk)
            yield

    def prompt_a(l, t, box, after_proj=None, mid_hook=None):
        kt_dst = (KT[:, :, t * 128:(t + 1) * 128], [("KT", t)])
        va_dst = (VA[:, t, :, 0:128], [("VA", t)], (OPT['PAIR'] and t % 2 == 1))
        g = stage_a(l, 128, box["h"], box["hk"], kt_dst, va_dst,
                    nkp[l, t * 128:(t + 1) * 128, :], nvp[l, t * 128:(t + 1) * 128, :], after_proj=after_proj,
                    mid_hook=mid_hook, hT_pre=box.get("hT"))
        late = (t - 1) >= OPT['COPY_DVE_T']
        early = (t - 1) < OPT['EVAC_ACT_T']
        while True:
            copy_on_dve[0] = late
            evac_on_act[0] = early
            try:
                r = next(g)
            except StopIteration:
                break
            finally:
                copy_on_dve[0] = False
                evac_on_act[0] = False
            if isinstance(r, dict):
                box["st"] = r
            yield

    def prompt_b(l, t, box):
        st = box["st"]
        QT_, QTk = st["QT"], st["QTk"]
        o_, ok = ob.next()
        groups = []
        for h in range(NH):
            j = 0
            while j <= t:
                if OPT['PAIR'] and j % 2 == 0 and j + 1 <= t:
                    groups.append([(h, j), (h, j + 1)])
                    j += 2
                else:
                    groups.append([(h, j)])
                    j += 1
        units = []
        cur, n = [], 0
        for g in groups:
            if n + len(g) > 4:
                units.append(cur)
                cur, n = [], 0
            cur.append(g)
            n += len(g)
        if cur:
            units.append(cur)
        Obuf = {}
        defer = []

        def flush():
            while defer:
                O_, Ok, h = defer.pop(0)
                finalize_head(O_, Ok, o_, ok, h, 128, nsum=3)

        def qk_unit(unit):
            S_, Sk = su.next()
            s = 0
            for g in unit:
                for (h, j) in g:
                    last = (j != t)
                    pe(lambda e, s=s, h=h, j=j, last=last: e.matmul(
                        S_[:, 0, s * 128:(s + 1) * 128], lhsT=KT[0:64, h, j * 128:(j + 1) * 128],
                        rhs=QT_[0:64, h, :], start=True, stop=last), r=[("KT", j), QTk], w=[Sk])
                    pe(lambda e, s=s, h=h, j=j, last=last: e.matmul(
                        S_[:, 1, s * 128:(s + 1) * 128], lhsT=KT[64:128, h, j * 128:(j + 1) * 128],
                        rhs=QT_[64:128, h, :], start=True, stop=last), r=[("KT", j), QTk], w=[Sk])
                    if j == t:
                        pe(lambda e, s=s, h=h: e.matmul(S_[:, 0, s * 128:(s + 1) * 128], lhsT=ident[:],
                                                        rhs=Dt[:, h, 0:128], start=False, stop=True),
                           r=["ident", "Dt"], w=[Sk])
                        pe(lambda e, s=s, h=h: e.matmul(S_[:, 1, s * 128:(s + 1) * 128], lhsT=ident[:],
                                                        rhs=Dt[:, h, 0:128], start=False, stop=True),
                           r=["ident", "Dt"], w=[Sk])
                    s += 1
            return S_, Sk

        def exp_unit(unit, S_, Sk):
            pts = []
            s = 0
            for g in unit:
                h, j0 = g[0]
                w_ = 128 * len(g)
                p_, pk = PT.next()
                act(lambda e, s=s, h=h, j0=j0, p_=p_, w_=w_: e.activation(
                    out=p_[:, :, 0:w_], in_=S_[:, :, s * 128:s * 128 + w_], func=AF.Exp,
                    bias=btab[:, h, (t - j0):(t - j0) + 1]), r=[Sk, "btab"], w=[pk])
                pts.append((p_, pk))
                s += len(g)
            return pts

        def av_unit(unit, pts):
            for g, (p_, pk) in zip(unit, pts):
                for i, (h, j) in enumerate(g):
                    if j == 0:
                        if len(defer) >= 2:
                            flush()
                        Obuf[h] = opb.next()
                    O_, Ok = Obuf[h]
                    pe(lambda e, h=h, j=j, i=i, p_=p_, O_=O_: e.matmul(
                        O_[:, 0:131], lhsT=p_[:, 0, i * 128:(i + 1) * 128], rhs=VA[:, j, h, 0:131],
                        start=(j == 0), stop=(j == t)), r=[pk, ("VA", j)], w=[Ok])
                    pe(lambda e, h=h, j=j, i=i, p_=p_, O_=O_: e.matmul(
                        O_[:, 256:387], lhsT=p_[:, 1, i * 128:(i + 1) * 128], rhs=VA[:, j, h, 0:131],
                        start=False, stop=(j == t), skip_group_check=True), r=[pk, ("VA", j)], w=[Ok])
                    if j == t:
                        if OPT['LAG']:
                            defer.append((O_, Ok, h))
                        else:
                            finalize_head(O_, Ok, o_, ok, h, 128, nsum=3)

        pend = []
        for unit in units:
            flush()
            S_, Sk = qk_unit(unit)
            pend.append((unit, exp_unit(unit, S_, Sk)))
            if len(pend) > 2:
                av_unit(*pend.pop(0))
            yield
        while pend:
            flush()
            av_unit(*pend.pop(0))
            yield
        flush()

        box.update(o=o_, ok=ok)

    def prompt_c(l, t, box):
        def x_store(xo_ap, xok):
            if l == L - 1:
                P.dma("sp", yp[t * 128:(t + 1) * 128, :], xo_ap, reads=[xok], final=True)
            else:
                P.dma("sp", xmid[t * 128:(t + 1) * 128, :], xo_ap, reads=[xok], writes=[("xmid", t)])

        for _ in stage_c(l, 128, box["st"], box["o"], box["ok"], box["X"], box["Xk"], x_store):
            yield

    def cache_load(l, b, h, box):
        kc_, kck = Kc.next()
        vs_, vsk = Vs.next()
        P.dma("pool", kc_[:], ck[l, b].rearrange("(j p) n -> p j n", p=128)[:, :, h * 128:(h + 1) * 128],
              writes=[kck])
        P.dma("pool", vs_[:, :, 0:128], cv[l, b].rearrange("(j p) n -> p j n", p=128)[:, :, h * 128:(h + 1) * 128],
              writes=[vsk])
        box[("cache", h)] = (kc_, kck, vs_, vsk)

    def sample_n(l, b, box):
        np_ = DEC
        Xt_, Xk = xt.next()
        X = Xt_[0:np_, :]
        if l == 0:
            P.dma("sp", X, xs[b], writes=[Xk])
        else:
            P.dma("sp", X, xsmid[b], reads=[("xsmid", b)], writes=[Xk])
        h_, hk, fin = stage_n(np_, X, Xk)
        box.update(X=X, Xk=Xk, h=h_, hk=hk)
        yield
        if fin is not None:
            fin()
            yield
        if OPT['TH_IN_N']:
            box["hT"] = stage_th(np_, h_, hk)
            yield

    def sample_a(l, b, box, after_proj=None):
        np_ = DEC
        kt_, ktk = KTn.next()
        vn_, vnk = Vn.next()
        box.update(kt=kt_, ktk=ktk, vn=vn_, vnk=vnk)
        g = stage_a(l, np_, box["h"], box["hk"], (kt_[:, :, :], [ktk]), (vn_[:, :, 0:128], [vnk]), nks[l, b], nvs[l, b],
                    sgu_out=nsg[l, b], after_proj=after_proj, hT_pre=box.get("hT"))
        for i, r in enumerate(g):
            if isinstance(r, dict):
                box["st"] = r
            yield

    def sample_b(l, b, box):
        np_ = DEC
        st = box["st"]
        kt_, ktk, vn_, vnk = box["kt"], box["ktk"], box["vn"], box["vnk"]
        QT_, QTk = st["QT"], st["QTk"]
        o_, ok = ob.next()
        cache_load(l, b, 0, box)
        cache_load(l, b, 1, box)
        for h in range(NH):
            kc_, kck, vs_, vsk = box[("cache", h)]
            tpt, tpk = tp.next()
            for j in range(8):
                pe(lambda e, j=j: e.transpose(tpt[:, j * 128:(j + 1) * 128], kc_[:, j, :], ident[:]),
                   r=[kck, "ident"], w=[tpk])
            kts_, ktsk = KTs.next()
            dve(lambda e: e.tensor_copy(kts_[:], tpt[:]), r=[tpk], w=[ktsk])
            yield
            sps, Sk_ = su.next()
            skeys = [Sk_]
            for j in range(8):
                for c in range(2):
                    pe(lambda e, j=j, c=c: e.matmul(sps[:, c, j * DEC:(j + 1) * DEC],
                                                    lhsT=kts_[c * 64:(c + 1) * 64, j * 128:(j + 1) * 128],
                                                    rhs=QT_[c * 64:(c + 1) * 64, h, 0:DEC], start=True, stop=True),
                       r=[ktsk, QTk], w=skeys)
            for c in range(2):
                pe(lambda e, c=c: e.matmul(sps[0:DEC, c, 128:128 + DEC], lhsT=kt_[c * 64:(c + 1) * 64, h, 0:DEC],
                                           rhs=QT_[c * 64:(c + 1) * 64, h, 0:DEC], start=True, stop=False),
                   r=[ktk, QTk], w=skeys)
            for c in range(2):
                pe(lambda e, c=c: e.matmul(sps[0:DEC, c, 128:128 + DEC], lhsT=ident[:, 0:DEC],
                                           rhs=Dt[:, h, 0:DEC], start=False, stop=True),
                   r=["ident", "Dt"], w=skeys)
            sb_, sbk = ssb.next()
            dve(lambda e: e.tensor_tensor(out=sb_[:], in0=sps[:, :, 0:128],
                                          in1=bsx0[:, h, :, :].rearrange("p j q -> p (j q)").unsqueeze(1).to_broadcast([128, 2, 128]),
                                          op=ALU.add), r=skeys + ["bsx0"], w=[sbk])
            p_, pk = PTs.next()
            act(lambda e: e.activation(out=p_[:], in_=sb_[:], func=AF.Exp, bias=negM[:, 0:1]), r=[sbk, "negM"], w=[pk])
            pn_, pnk = PTn.next()
            act(lambda e: e.activation(out=pn_[:], in_=sps[0:DEC, :, 128:128 + DEC], func=AF.Exp,
                                       bias=btab[0:DEC, h, 0:1]), r=skeys + ["btab"], w=[pnk])
            yield
            O_, Ok = opb.next()
            for c in range(2):
                for j in range(8):
                    pe(lambda e, j=j, c=c: e.matmul(O_[0:DEC, c * 256:c * 256 + 129], lhsT=p_[:, c, j * DEC:(j + 1) * DEC],
                                                    rhs=vs_[:, j, 0:129], start=(c == 0 and j == 0), stop=False,
                                                    skip_group_check=True), r=[pk, vsk], w=[Ok])
                pe(lambda e, c=c: e.matmul(O_[0:DEC, c * 256:c * 256 + 129], lhsT=pn_[:, c, :], rhs=vn_[:, h, 0:129],
                                           start=False, stop=True, skip_group_check=True), r=[pnk, vnk], w=[Ok])
            finalize_head(O_, Ok, o_, ok, h, DEC)
            if h + 2 < NH:
                cache_load(l, b, h + 2, box)
            yield

        box.update(o=o_, ok=ok)

    def sample_c(l, b, box):
        def x_store(xo_ap, xok):
            if l == L - 1:
                P.dma("sp", ys[b], xo_ap, reads=[xok], final=True)
            else:
                P.dma("sp", xsmid[b], xo_ap, reads=[xok], writes=[("xsmid", b)])

        for _ in stage_c(l, DEC, box["st"], box["o"], box["ok"], box["X"], box["Xk"], x_store):
            yield

    def run(g):
        for _ in g:
            pass

    def zipgen(gens):
        gl_ = [g for g, n in gens]
        perm = {0: None, 1: (1, 0, 2, 3), 2: (2, 0, 1, 3), 3: (3, 0, 1, 2), 4: (0, 2, 1, 3), 5: (3, 1, 0, 2)}[OPT['ORDER']]
        if perm is not None and len(gl_) == 4:
            gl_ = [gl_[p] for p in perm]
        while gl_:
            for g in list(gl_):
                try:
                    next(g)
                except StopIteration:
                    gl_.remove(g)

    def load_w_in(l):
        src_ = w_in[l].rearrange("(kt p) n -> p kt n", p=128)
        for c in CHUNK_ORDER:
            P.dma("pool", win[:, :, c * 512:(c + 1) * 512], src_[:, :, c * 512:(c + 1) * 512],
                  writes=[("win", c)])

    def load_w_out(l):
        src2 = w_out[l].rearrange("(kt p) n -> p kt n", p=128)
        for n in range(2):
            P.dma("pool", wout[:, :, n * 512:(n + 1) * 512], src2[:, :, n * 512:(n + 1) * 512],
                  writes=[("wout", n)])

    tiles = []
    for l in range(L):
        for t in range(NT):
            tiles.append((l, "p", t))
        for b in range(NS):
            tiles.append((l, "s", b))
    boxes = [dict() for _ in tiles]

    def gen_n(i):
        l, kind, x = tiles[i]
        return prompt_n(l, x, boxes[i]) if kind == "p" else sample_n(l, x, boxes[i])

    def gen_a(i):
        l, kind, x = tiles[i]
        hook = None
        if i + 1 < len(tiles) and tiles[i + 1][0] != l:
            src_ = w_in[l + 1].rearrange("(kt p) n -> p kt n", p=128)

            def hook(c):
                P.dma("pool", win[:, :, c * 512:(c + 1) * 512], src_[:, :, c * 512:(c + 1) * 512],
                      writes=[("win", c)])
        mid = None
        if i == 0 or tiles[i - 1][0] != l:
            if i == 0:
                mid = (lambda l=l: (load_params_a2(l), load_w_out(l)))
            else:
                mid = (lambda l=l: load_params_a2(l))
        return prompt_a(l, x, boxes[i], hook, mid) if kind == "p" else sample_a(l, x, boxes[i], hook)

    def gen_b(i):
        l, kind, x = tiles[i]
        return prompt_b(l, x, boxes[i]) if kind == "p" else sample_b(l, x, boxes[i])

    def gen_c(i):
        l, kind, x = tiles[i]
        return prompt_c(l, x, boxes[i]) if kind == "p" else sample_c(l, x, boxes[i])

    def layer_of(i):
        return tiles[i][0] if 0 <= i < len(tiles) else None

    NTT = len(tiles)
    load_params_n(0)
    setup_min()
    run(gen_n(0))
    if NTT > 1:
        run(gen_n(1))
    load_w_in(0)
    setup_consts()
    load_params_a(0)
    load_params_b(0)
    run(gen_a(0))
    for i in range(NTT + 1):
        gens = []
        if i < NTT:
            l_, kind_, x_ = tiles[i]
            nb = (x_ + 1) + 1 if kind_ == "p" else 3 * NH
            gens.append((gen_b(i), nb))
        if i + 1 < NTT:
            if layer_of(i + 1) != layer_of(i):
                load_params_a(layer_of(i + 1))
            gens.append((gen_a(i + 1), 11))
        if i >= 1:
            gens.append((gen_c(i - 1), 5))
        if i + 2 < NTT:
            if layer_of(i + 2) != layer_of(i + 1):
                load_params_n(layer_of(i + 2))
            gens.append((gen_n(i + 2), 1))
        zipgen(gens)
        if 1 <= i < NTT and layer_of(i) != layer_of(i - 1):
            load_w_out(layer_of(i))
        if i + 1 < NTT and layer_of(i + 1) != layer_of(i):
            load_params_b(layer_of(i + 1))

    sems = {e: es.enter_context(nc.semaphore("s_" + e)) for e in ENGS}
    dsems = {e: [es.enter_context(nc.semaphore(f"d_{e}{i}")) for i in range(NDMA_SEM)] for e in ("sp", "pool")}
    for e in ENGS:
        dsems.setdefault(e, [])
    with nc.Block() as block:
        @block.tensor
        def _(e):
            P.emit_engine("pe", e, sems, dsems)

        @block.scalar
        def _(e):
            P.emit_engine("act", e, sems, dsems)

        @block.vector
        def _(e):
            P.emit_engine("dve", e, sems, dsems)

        @block.gpsimd
        def _(e):
            P.emit_engine("pool", e, sems, dsems)

        @block.sync
        def _(e):
            P.emit_engine("sp", e, sems, dsems)
    es.close()
    return nc, P


_CACHE = {}


def _get_program(T, L, NS):
    key = (T, L, NS)
    if key not in _CACHE:
        _CACHE[key] = build_program(T, L, NS)[0]
    return _CACHE[key]


def kernel(x_prompt, x_sample, cache_k, cache_v, norm_g, w_in, sgu_norm_g, sgu_w, sgu_b,
           q_norm_g, k_norm_g, lambda_q1, lambda_k1, lambda_q2, lambda_k2, subln_g, w_out):
    f = lambda a: np.ascontiguousarray(np.asarray(a, dtype=np.float32))
    x_prompt, x_sample, cache_k, cache_v = f(x_prompt), f(x_sample), f(cache_k), f(cache_v)
    B, T, _ = x_prompt.shape
    L = w_in.shape[0]
    n_cores = B
    NS = x_sample.shape[0] // n_cores
    nc = _get_program(T, L, NS)
    shared = {
        "norm_g": f(norm_g), "w_in": f(w_in), "sgu_norm_g": f(sgu_norm_g).reshape(L, 512), "sgu_w": f(sgu_w),
        "sgu_b": f(sgu_b), "q_norm_g": f(q_norm_g), "k_norm_g": f(k_norm_g), "lq1": f(lambda_q1),
        "lk1": f(lambda_k1), "lq2": f(lambda_q2), "lk2": f(lambda_k2), "subln_g": f(subln_g), "w_out": f(w_out),
    }
    ckr = cache_k.reshape(L, B * NS, PAST, 512)
    cvr = cache_v.reshape(L, B * NS, PAST, 512)
    in_maps = []
    for c in range(n_cores):
        m = dict(shared)
        m["xp"] = x_prompt[c]
        m["xs"] = x_sample[c * NS:(c + 1) * NS]
        m["ck"] = np.ascontiguousarray(ckr[:, c * NS:(c + 1) * NS])
        m["cv"] = np.ascontiguousarray(cvr[:, c * NS:(c + 1) * NS])
        in_maps.append(m)
    res = run_bass_kernel_spmd(nc, in_maps, core_ids=list(range(n_cores)))
    rs = res.results
    y_prompt = np.stack([r["yp"] for r in rs], axis=0)
    y_sample = np.concatenate([r["ys"] for r in rs], axis=0)
    nk_p = np.stack([r["nkp"] for r in rs], axis=1).reshape(L, B, T, NH, 2, 64)
    nv_p = np.stack([r["nvp"] for r in rs], axis=1).reshape(L, B, T, NH, 128)
    nk_s = np.concatenate([r["nks"] for r in rs], axis=1).reshape(L, B * NS, DEC, NH, 2, 64)
    nv_s = np.concatenate([r["nvs"] for r in rs], axis=1).reshape(L, B * NS, DEC, NH, 128)
    ns_g = np.concatenate([r["nsg"] for r in rs], axis=1).reshape(L, B * NS, DEC, 512)
    return (y_prompt.astype(np.float32), y_sample.astype(np.float32), nk_p.astype(np.float32),
            nv_p.astype(np.float32), nk_s.astype(np.float32), nv_s.astype(np.float32), ns_g.astype(np.float32))
```

```python
import math
from contextlib import ExitStack

import numpy as np
import concourse.bass as bass
import concourse.mybir as mybir
from concourse.bass_utils import run_bass_kernel_spmd

F32 = mybir.dt.float32
BF16 = mybir.dt.bfloat16
AF = mybir.ActivationFunctionType
ALU = mybir.AluOpType
AX = mybir.AxisListType

ENGS = ("pe", "act", "dve", "pool", "sp")
_SBUF_FREE = [0]
OPT = dict((('TH_IN_N', 0), ('YIELD_YT', 1), ('YIELD_C0', 1), ('YIELD_QT', 0), ('LAG', 1), ('PAIR', 1), ('YIELD_TH', 1), ('C_POOL', 0), ('VA_POOL', 1), ('ORDER', 0), ('COPY_DVE_T', 99), ('EVAC_ACT_T', 0)))
NDMA_SEM = 8

D_MODEL = 1024
IN_W = 3584
NH = 4
EPS = 1e-6
PAST = 1024
DEC = 16
SLOPES = [2.0 ** (-8.0 * (h + 1) / NH) for h in range(NH)]
NEG = -30000.0
C_U, C_VA, C_GA, C_Q, C_K, C_V, C_GB = range(7)


class Op:
    __slots__ = ("eng", "fn", "dma", "deps", "sig", "signum", "qidx")

    def __init__(self, eng, fn, dma):
        self.eng = eng
        self.fn = fn
        self.dma = dma
        self.deps = set()
        self.sig = False
        self.signum = None
        self.qidx = None


class _Rec:
    def __init__(self):
        self.call = None

    def __getattr__(self, name):
        def f(*a, **k):
            self.call = (name, a, k)
            return self
        return f


class Prog:
    def __init__(self):
        self.ops = {e: [] for e in ENGS}
        self.res = {}
        self.dma_ops = {e: [] for e in ENGS}
        self.final_waits = []
        self.prepared = False
        self.nwaits = 0

    def _st(self, key):
        st = self.res.get(key)
        if st is None:
            st = [None, []]
            self.res[key] = st
        return st

    def op(self, eng, fn, reads=(), writes=(), dma=False):
        rec = _Rec()
        fn(rec)
        assert rec.call is not None
        o = Op(eng, rec.call, dma)
        deps = set()
        for r in reads:
            st = self._st(r)
            if st[0] is not None:
                deps.add((st[0], 0))
        for w in writes:
            st = self._st(w)
            if st[0] is not None:
                deps.add((st[0], 1))
            for rd in st[1]:
                deps.add((rd, 2))
        for key in tuple(reads) + tuple(writes):
            if isinstance(key, tuple) and key and key[0] == "P":
                st = self._st(key)
                for rd in st[1]:
                    if rd.eng != eng:
                        deps.add((rd, 3))
        for d, kind in deps:
            if d is o:
                continue
            if (not d.dma) and (not dma) and d.eng == eng and eng == "pe":
                continue
            o.deps.add(d)
        if dma:
            q = self.dma_ops[eng]
            o.qidx = len(q)
            if o.qidx >= NDMA_SEM:
                o.deps.add(q[o.qidx - NDMA_SEM])
            q.append(o)
        for r in reads:
            self._st(r)[1].append(o)
        for w in writes:
            st = self._st(w)
            st[0] = o
            st[1] = []
        self.ops[eng].append(o)
        return o

    def dma(self, eng, out, in_, reads=(), writes=(), final=False, **kw):
        def fn(e):
            return e.dma_start(out=out, in_=in_, **kw)
        o = self.op(eng, fn, reads, writes, dma=True)
        if final:
            self.final_waits.append(o)
        return o

    def prepare(self):
        for e in ENGS:
            for o in self.ops[e]:
                for d in o.deps:
                    d.sig = True
        for o in self.final_waits:
            o.sig = True
        for e in ENGS:
            n = 0
            for o in self.ops[e]:
                if o.dma:
                    continue
                if o.sig:
                    n += 1
                    o.signum = n
        self.prepared = True

    def emit_engine(self, e, eng, sems, dsems):
        if not self.prepared:
            self.prepare()

        def target(d):
            if d.dma:
                return (dsems[d.eng][d.qidx % NDMA_SEM], 16 * (d.qidx // NDMA_SEM + 1))
            return (sems[d.eng], d.signum)

        wm = {}
        for o in self.ops[e]:
            need = {}
            for d in o.deps:
                s, v = target(d)
                k = id(s)
                if wm.get(k, 0) >= v:
                    continue
                if k not in need or need[k][1] < v:
                    need[k] = (s, v)
            for k, (s, v) in need.items():
                eng.wait_ge(s, v)
                wm[k] = v
                self.nwaits += 1
            name_, a_, k_ = o.fn
            ins = getattr(eng, name_)(*a_, **k_)
            if o.dma:
                ins.then_inc(dsems[e][o.qidx % NDMA_SEM], 16)
            elif o.sig:
                ins.then_inc(sems[e], 1)
        if e == "sp":
            for o in self.final_waits:
                s, v = target(o)
                if wm.get(id(s), 0) >= v:
                    continue
                eng.wait_ge(s, v)
                wm[id(s)] = v


def _bf16_round(x):
    u = np.array([x], dtype=np.float32).view(np.uint32)
    u = ((u + np.uint32(0x7FFF) + ((u >> np.uint32(16)) & np.uint32(1))) & np.uint32(0xFFFF0000)).astype(np.uint32)
    return float(u.view(np.float32)[0])


def _bf16_split3(c):
    hi = _bf16_round(c)
    mid = _bf16_round(c - hi)
    lo = _bf16_round(c - hi - mid)
    return hi, mid, lo


class Rot:
    def __init__(self, alloc, name, n, shape, dt):
        self.bufs = [alloc(f"{name}{i}", shape, dt) for i in range(n)]
        self.name = name
        self.i = -1

    def next(self):
        self.i = (self.i + 1) % len(self.bufs)
        return self.bufs[self.i], (self.name, self.i)


def build_program(T=2048, L=2, NS=2, interleave=True):
    NT = T // 128
    nc = bass.Bass("TRN2", target_bir_lowering=False)

    def din(name, shape):
        return nc.dram_tensor(name, list(shape), F32, kind="ExternalInput").ap()

    def dout(name, shape):
        return nc.dram_tensor(name, list(shape), F32, kind="ExternalOutput").ap()

    xp = din("xp", [T, D_MODEL])
    xs = din("xs", [NS, DEC, D_MODEL])
    ck = din("ck", [L, NS, PAST, 512])
    cv = din("cv", [L, NS, PAST, 512])
    norm_g = din("norm_g", [L, D_MODEL])
    w_in = din("w_in", [L, D_MODEL, IN_W])
    sgu_norm_g = din("sgu_norm_g", [L, 512])
    sgu_w = din("sgu_w", [L, NH, 128, 128])
    sgu_b = din("sgu_b", [L, NH, 128])
    q_norm_g = din("q_norm_g", [L, 64])
    k_norm_g = din("k_norm_g", [L, 64])
    lq1 = din("lq1", [L, 64])
    lk1 = din("lk1", [L, 64])
    lq2 = din("lq2", [L, 64])
    lk2 = din("lk2", [L, 64])
    subln_g = din("subln_g", [L, 128])
    w_out = din("w_out", [L, D_MODEL, D_MODEL])

    yp = dout("yp", [T, D_MODEL])
    ys = dout("ys", [NS, DEC, D_MODEL])
    nkp = dout("nkp", [L, T, 512])
    nvp = dout("nvp", [L, T, 512])
    nks = dout("nks", [L, NS, DEC, 512])
    nvs = dout("nvs", [L, NS, DEC, 512])
    nsg = dout("nsg", [L, NS, DEC, 512])
    xmid = nc.dram_tensor("xmid", [T, D_MODEL], F32, kind="Internal").ap()
    xsmid = nc.dram_tensor("xsmid", [NS, DEC, D_MODEL], F32, kind="Internal").ap()

    P = Prog()
    es = ExitStack()

    def sb(name, shape, dt=F32):
        return es.enter_context(nc.sbuf_tensor(name, list(shape), dt))

    def ps(name, shape, dt=F32):
        return es.enter_context(nc.psum_tensor(name, list(shape), dt))

    win = sb("win", [128, 8, IN_W], BF16)
    wout = sb("wout", [128, 8, D_MODEL], BF16)
    KT = sb("KT", [128, NH, T], BF16)
    VA = sb("VA", [128, NT, NH, 132], BF16)
    ident = sb("ident", [128, 128], BF16)
    Dt = sb("Dt", [128, NH, 128], BF16)
    mhalf = sb("mhalf", [128, 8], F32)
    vsc = sb("vsc", [128, NH], F32)
    btab0 = sb("btab0", [128, NH, 16], F32)
    btab = sb("btab", [128, NH, 16], F32)
    bsx0 = sb("bsx0", [128, NH, 8, DEC], F32)
    gN = sb("gN", [128, D_MODEL], F32)
    gS = sb("gS", [128, 512], F32)
    gqc = sb("gqc", [128, 1], F32)
    gkc = sb("gkc", [128, 1], F32)
    gl = sb("gl", [128, 128], F32)
    WT = sb("WT", [128, NH, 128], BF16)
    bS = sb("bS", [128, NH], F32)
    nlam = sb("nlam", [128, 1], F32)
    Mb = sb("Mb", [128, 1], F32)
    small = sb("small", [128, 7, 64], F32)
    gLr = sb("gLr", [128, 128], F32)
    sc = sb("sc", [128, 16], F32)
    negM = sb("negM", [128, 1], F32)

    xt = Rot(sb, "xt", 4, [128, D_MODEL], F32)
    hb = Rot(sb, "hb", 2, [128, D_MODEL], BF16)
    hT = Rot(sb, "hT", 2, [128, 8, 128], BF16)
    sq = Rot(sb, "sq", 2, [128, 512], F32)
    zr = Rot(sb, "zr", 3, [128, 512], F32)
    stat = Rot(sb, "stat", 8, [128, 8], F32)
    rst = Rot(sb, "rst", 8, [128, 8], F32)
    ta = Rot(sb, "ta", 1, [128, 512], F32)
    van = Rot(sb, "van", 1, [128, 512], BF16)
    qn = Rot(sb, "qn", 1, [128, 512], BF16)
    knb = Rot(sb, "knb", 1, [128, 512], BF16)
    vf = Rot(sb, "vf", 1, [128, 512], F32)
    QT = Rot(sb, "QT", 2, [128, NH, 128], BF16)
    tb = Rot(sb, "tb", 3, [128, 512], F32)
    PT = Rot(sb, "PT", 6, [128, 2, 256], BF16) if OPT["PAIR"] else Rot(sb, "PT", 12, [128, 2, 128], BF16)
    ob = Rot(sb, "ob", 2, [128, 512], F32)
    t1 = Rot(sb, "t1", 2, [128, 128], F32)
    r2 = Rot(sb, "r2", 4, [128, 4], F32)
    yb = Rot(sb, "yb", 3, [128, D_MODEL], BF16)
    yT = Rot(sb, "yT", 1, [128, 8, 128], BF16)
    xo = Rot(sb, "xo", 1, [128, D_MODEL], F32)
    Kc = Rot(sb, "Kc", 2, [128, 8, 128], BF16)
    KTs = Rot(sb, "KTs", 1, [128, 8 * 128], BF16)
    Vs = Rot(sb, "Vs", 2, [128, 8, 130], BF16)
    KTn = Rot(sb, "KTn", 2, [128, NH, DEC], BF16)
    Vn = Rot(sb, "Vn", 2, [DEC, NH, 130], BF16)
    ssb = Rot(sb, "ssb", 2, [128, 2, 128], F32)
    PTs = Rot(sb, "PTs", 2, [128, 2, 128], BF16)
    PTn = Rot(sb, "PTn", 2, [DEC, 2, DEC], BF16)

    wf32 = sq.bufs[0][:].rearrange("p (h s) -> p h s", h=NH)
    wbf_t = sb("wbf_t", [128, NH * 128], BF16)
    wbf = wbf_t[:].rearrange("p (h s) -> p h s", h=NH)
    WF = ("sq", 0)
    WB = "wbf_t"
    class PRot:
        def __init__(self, tag, n, shape):
            self.bufs = [ps(f"psum_{tag}{i}", shape, F32) for i in range(n)]
            self.tag = tag
            self.i = -1

        def next(self):
            self.i = (self.i + 1) % len(self.bufs)
            return self.bufs[self.i], ("P", self.tag, self.i)

    zp = PRot("zb", 2, [128, 512])
    su = PRot("su", 2, [128, 2, 512])
    opb = PRot("ob", 2, [128, 512])

    class TP:
        def next(self):
            z_, zk = zp.next()
            return z_[:].bitcast(BF16), zk

    tp = TP()

    _SBUF_FREE[0] = nc.sbuf_bytes_remaining
    def act(fn, r=(), w=()):
        return P.op("act", fn, r, w)

    def dve(fn, r=(), w=()):
        return P.op("dve", fn, r, w)

    def pool(fn, r=(), w=()):
        return P.op("pool", fn, r, w)

    def pe(fn, r=(), w=()):
        return P.op("pe", fn, r, w)

    def setup_min():
        idf = wf32
        pool(lambda e: e.memset(mhalf[:], -0.5), w=["mhalf"])
        pool(lambda e: e.memset(idf[:, 0, :], 0.0), w=[WF])
        pool(lambda e: e.affine_select(out=idf[:, 0, :], in_=idf[:, 0, :], pattern=[[-1, 128]],
                                       compare_op=ALU.not_equal, fill=1.0, base=0, channel_multiplier=1),
             r=[WF], w=[WF])
        dve(lambda e: e.tensor_copy(ident[:], idf[:, 0, :]), r=[WF], w=["ident"])

    def setup_consts():
        idf = wf32
        pool(lambda e: e.memset(VA[:, :, :, 128:132], 0.0), w=[("VA", j) for j in range(NT)])
        pool(lambda e: e.memset(VA[:, :, :, 128:129], 1.0), w=[("VA", j) for j in range(NT)])
        for h in range(NH):
            hi, mid, lo = _bf16_split3(math.exp(128.0 * SLOPES[h]))
            cval = float(np.float32(np.float32(hi) + np.float32(mid) + np.float32(lo)))
            pool(lambda e, h=h, cval=cval: e.memset(vsc[:, h:h + 1], cval), w=["vsc"])
            for j in (range(1, NT, 2) if OPT['PAIR'] else ()):
                for ci, cv_ in enumerate((hi, mid, lo)):
                    pool(lambda e, h=h, j=j, ci=ci, cv_=cv_: e.memset(VA[:, j, h, 128 + ci:129 + ci], cv_),
                         w=[("VA", j)])
        for i in range(2):
            b_ = Vs.bufs[i]
            pool(lambda e, b_=b_: e.memset(b_[:, :, 128:130], 1.0), w=[("Vs", i)])
        for i in range(2):
            b_ = Vn.bufs[i]
            pool(lambda e, b_=b_: e.memset(b_[:, :, 128:130], 1.0), w=[("Vn", i)])
        pool(lambda e: e.iota(idf[:, 1, :], pattern=[[-1, 128]], base=0, channel_multiplier=1,
                              allow_small_or_imprecise_dtypes=True), r=["ident"], w=[WF])
        for h in range(NH):
            dve(lambda e, h=h: e.tensor_scalar(out=idf[:, 2, :], in0=idf[:, 1, :], scalar1=0.0,
                                               scalar2=-2.0 * SLOPES[h], op0=ALU.max, op1=ALU.mult),
                r=[WF], w=[WF])
            dve(lambda e: e.memset(idf[64:128, 2, 0:64], NEG), r=[WF], w=[WF])
            dve(lambda e, h=h: e.tensor_copy(Dt[:, h, 0:128], idf[:, 2, :]), r=[WF], w=["Dt"])
        pool(lambda e: e.iota(idf[:, 3, 0:16], pattern=[[-128, 16]], base=0, channel_multiplier=1,
                              allow_small_or_imprecise_dtypes=True), r=["Dt"], w=[WF])
        for h in range(NH):
            dve(lambda e, h=h: e.tensor_scalar(out=btab0[:, h, :], in0=idf[:, 3, 0:16], scalar1=SLOPES[h],
                                               scalar2=None, op0=ALU.mult), r=[WF], w=["btab0"])
        pool(lambda e: e.iota(idf[:, 3, :].rearrange("p (j q) -> p j q", q=DEC), pattern=[[128, 8], [0, DEC]],
                              base=-PAST, channel_multiplier=1, allow_small_or_imprecise_dtypes=True),
             r=["btab0"], w=[WF])
        for h in range(NH):
            dve(lambda e, h=h: e.tensor_scalar(out=bsx0[:, h, :, :].rearrange("p j q -> p (j q)"),
                                               in0=idf[:, 3, :], scalar1=SLOPES[h], scalar2=None, op0=ALU.mult),
                r=[WF], w=["bsx0"])

    CHUNK_ORDER = [C_Q, C_K, C_V, C_GA, C_VA, C_U, C_GB]

    def load_weights(l):
        src = w_in[l].rearrange("(kt p) n -> p kt n", p=128)
        for c in CHUNK_ORDER:
            P.dma("pool", win[:, :, c * 512:(c + 1) * 512], src[:, :, c * 512:(c + 1) * 512],
                  writes=[("win", c)])
        src2 = w_out[l].rearrange("(kt p) n -> p kt n", p=128)
        for n in range(2):
            P.dma("pool", wout[:, :, n * 512:(n + 1) * 512], src2[:, :, n * 512:(n + 1) * 512],
                  writes=[("wout", n)])

    def load_params_n(l):
        P.dma("sp", gN[:], norm_g[l:l + 1, :].partition_broadcast(128), writes=["gN"])

    def load_params_a(l):
        lam_init = 0.8 - 0.6 * math.exp(-0.3 * l)
        P.dma("sp", gS[:], sgu_norm_g[l:l + 1, :].partition_broadcast(128), writes=["gS"])
        for i, t_ in enumerate((q_norm_g, k_norm_g)):
            P.dma("sp", small[:, i, :], t_[l:l + 1, :].partition_broadcast(128), writes=[("small", i)])
        P.dma("sp", gLr[:], subln_g[l:l + 1, :].partition_broadcast(128), writes=["gLr"])
        for half in range(2):
            P.dma("sp", gqc[half * 64:(half + 1) * 64, :], q_norm_g[l].rearrange("(d o) -> d o", o=1), writes=["gqc"],
                  allow_slow_non_contiguous=True)
            P.dma("sp", gkc[half * 64:(half + 1) * 64, :], k_norm_g[l].rearrange("(d o) -> d o", o=1), writes=["gkc"],
                  allow_slow_non_contiguous=True)
        dve(lambda e: e.tensor_scalar(out=gqc[:], in0=gqc[:], scalar1=0.125, scalar2=None, op0=ALU.mult),
            r=["gqc"], w=["gqc"])
        dve(lambda e: e.tensor_scalar(out=gl[:], in0=gLr[:], scalar1=0.5 * (1.0 - lam_init), scalar2=None,
                                      op0=ALU.mult), r=["gLr"], w=["gl"])
        dve(lambda e: e.tensor_reduce(out=sc[:, 0:1], in_=small[:, 0, :], axis=AX.X, op=ALU.max,
                                      apply_absolute_value=True), r=[("small", 0)], w=["sc0"])
        dve(lambda e: e.tensor_reduce(out=sc[:, 1:2], in_=small[:, 1, :], axis=AX.X, op=ALU.max,
                                      apply_absolute_value=True), r=[("small", 1)], w=["sc1"])
        dve(lambda e: e.tensor_scalar(out=Mb[:], in0=sc[:, 0:1], scalar1=sc[:, 1:2], scalar2=8.0,
                                      op0=ALU.mult, op1=ALU.mult), r=["sc0", "sc1"], w=["Mb"])
    def load_params_a2(l):
        P.dma("sp", wf32[:], sgu_w[l].rearrange("h t s -> t h s"), writes=[WF])
        P.dma("sp", bS[:], sgu_b[l].rearrange("h t -> t h"), writes=["bS"], allow_slow_non_contiguous=True)
        for h in range(NH):
            pool(lambda e, h=h: e.affine_select(out=wf32[:, h, :], in_=wf32[:, h, :], pattern=[[-1, 128]],
                                                compare_op=ALU.is_ge, fill=0.0, base=0, channel_multiplier=1),
                 r=[WF], w=[WF])
        dve(lambda e: e.tensor_scalar(out=wbf[:].rearrange("p h s -> p (h s)"),
                                      in0=wf32[:].rearrange("p h s -> p (h s)"), scalar1=0.5, scalar2=None,
                                      op0=ALU.mult), r=[WF], w=[WB])
        tpt, tpk = tp.next()
        for h in range(NH):
            pe(lambda e, h=h: e.transpose(tpt[:, h * 128:(h + 1) * 128], wbf[:, h, :], ident[:]),
               r=[WB, "ident"], w=[tpk])
        dve(lambda e: e.tensor_copy(WT[:].rearrange("p h t -> p (h t)"), tpt[:, 0:512]), r=[tpk], w=["WT"])
        dve(lambda e: e.tensor_scalar(out=bS[:], in0=bS[:], scalar1=0.5, scalar2=None, op0=ALU.mult),
            r=["bS"], w=["bS"])

    def load_params_b(l):
        lam_init = 0.8 - 0.6 * math.exp(-0.3 * l)
        dve(lambda e: e.tensor_scalar(out=btab[:].rearrange("p h d -> p (h d)"),
                                      in0=btab0[:].rearrange("p h d -> p (h d)"), scalar1=Mb[:, 0:1],
                                      scalar2=None, op0=ALU.subtract), r=["Mb", "btab0"], w=["btab"])
        dve(lambda e: e.tensor_scalar(out=negM[:], in0=Mb[:], scalar1=-1.0, scalar2=None, op0=ALU.mult),
            r=["Mb"], w=["negM"])
        for i, (a_, b_) in enumerate(((lq1, lk1), (lq2, lk2))):
            sa, sb_ = 2 + 2 * i, 3 + 2 * i
            P.dma("sp", small[:, sa, :], a_[l:l + 1, :].partition_broadcast(128), writes=[("small", sa)])
            P.dma("sp", small[:, sb_, :], b_[l:l + 1, :].partition_broadcast(128), writes=[("small", sb_)])
            dve(lambda e, sa=sa, sb_=sb_: e.tensor_tensor(out=small[:, 6, :], in0=small[:, sa, :], in1=small[:, sb_, :],
                                                          op=ALU.mult),
                r=[("small", sa), ("small", sb_)], w=[("small", 6)])
            dve(lambda e, i=i: e.tensor_reduce(out=sc[:, 2 + i:3 + i], in_=small[:, 6, :], axis=AX.X, op=ALU.add),
                r=[("small", 6)], w=[f"sc{2 + i}"])
            act(lambda e, i=i: e.activation(out=sc[:, 4 + i:5 + i], in_=sc[:, 2 + i:3 + i], func=AF.Exp),
                r=[f"sc{2 + i}"], w=[f"sc{4 + i}"])
        dve(lambda e: e.tensor_scalar(out=nlam[:], in0=sc[:, 5:6], scalar1=sc[:, 4:5], scalar2=-lam_init,
                                      op0=ALU.subtract, op1=ALU.add), r=["sc4", "sc5"], w=["nlam"])

    def rsqrt_cols(src_ap, n, np_, scale, rkeys):
        st_, stk = stat.next()
        rs_, rsk = rst.next()
        pool(lambda e: e.tensor_scalar(out=st_[0:np_, 0:n], in0=src_ap, scalar1=scale, scalar2=EPS,
                                       op0=ALU.mult, op1=ALU.add), r=rkeys, w=[stk])
        pool(lambda e: e.tensor_tensor(out=rs_[0:np_, 0:n], in0=st_[0:np_, 0:n], in1=mhalf[0:np_, 0:n], op=ALU.pow),
             r=[stk, "mhalf"], w=[rsk])
        return rs_[0:np_, 0:n], rsk

    copy_on_dve = [False]
    evac_on_act = [False]

    def evac(out_ap, in_ap, rkeys, wkeys, scale_ap=None, scale_key=None):
        if evac_on_act[0]:
            if scale_ap is None:
                act(lambda e: e.activation(out=out_ap, in_=in_ap, func=AF.Copy), r=rkeys, w=wkeys)
            else:
                act(lambda e: e.activation(out=out_ap, in_=in_ap, func=AF.Copy, scale=scale_ap), r=rkeys + [scale_key], w=wkeys)
        else:
            if scale_ap is None:
                dve(lambda e: e.tensor_copy(out_ap, in_ap), r=rkeys, w=wkeys)
            else:
                dve(lambda e: e.tensor_scalar(out=out_ap, in0=in_ap, scalar1=scale_ap, scalar2=None, op0=ALU.mult),
                    r=rkeys + [scale_key], w=wkeys)

    def gn_stats(z_, zk, np_, ng, gs):
        r_, rk = zr.next()
        if copy_on_dve[0]:
            dve(lambda e: e.tensor_copy(r_[0:np_, :], z_[0:np_, :]), r=[zk], w=[rk])
        else:
            act(lambda e: e.activation(out=r_[0:np_, :], in_=z_[0:np_, :], func=AF.Copy), r=[zk], w=[rk])
        s_, sk = sq.next()
        act(lambda e: e.activation(out=s_[0:np_, :], in_=z_[0:np_, :], func=AF.Square), r=[zk], w=[sk])
        st_, stk = stat.next()
        dve(lambda e: e.tensor_reduce(out=st_[0:np_, 0:ng], in_=s_[0:np_, :].rearrange("p (a b) -> p a b", b=gs),
                                      axis=AX.X, op=ALU.add), r=[sk], w=[stk])
        rs_ap, rsk = rsqrt_cols(st_[0:np_, 0:ng], ng, np_, 1.0 / gs, [stk])
        return r_, rk, rs_ap, rsk

    def gn_apply(eng_fn, zr_, zrk, rs_ap, rsk, np_, ng, gs, out_ap, out_keys):
        eng_fn(lambda e: e.tensor_tensor(out=out_ap.rearrange("p (a b) -> p a b", b=gs),
                                         in0=zr_[0:np_, :].rearrange("p (a b) -> p a b", b=gs),
                                         in1=rs_ap.unsqueeze(2).to_broadcast([np_, ng, gs]), op=ALU.mult),
               r=[zrk, rsk], w=out_keys)

    def stage_n(np_, X, Xk):
        st_, stk = stat.next()
        h_, hk = hb.next()
        act(lambda e: e.activation(out=h_[0:np_, :], in_=X, func=AF.Square, accum_out=st_[0:np_, 0:1]),
            r=[Xk], w=[hk, stk])
        rs_ap, rsk = rsqrt_cols(st_[0:np_, 0:1], 1, np_, 1.0 / D_MODEL, [stk])

        def apply():
            dve(lambda e: e.scalar_tensor_tensor(out=h_[0:np_, :], in0=X, scalar=rs_ap, in1=gN[0:np_, :],
                                                 op0=ALU.mult, op1=ALU.mult), r=[Xk, rsk, "gN"], w=[hk])
        if OPT['LAG']:
            return h_, hk, apply
        apply()
        return h_, hk, None

    def stage_th(np_, h_, hk):
        tpt, tpk = tp.next()
        for k in range(8):
            pe(lambda e, k=k: e.transpose(tpt[:, k * np_:(k + 1) * np_], h_[0:np_, k * 128:(k + 1) * 128],
                                          ident[0:np_, 0:np_]), r=[hk, "ident"], w=[tpk])
        hT_, hTk = hT.next()
        evac(hT_[:, :, 0:np_], tpt[:, 0:8 * np_].rearrange("p (k t) -> p k t", t=np_), [tpk], [hTk])
        return hT_, hTk

    def stage_a(l, np_, h_, hk, kt_dst, va_dst, k_out, v_out, sgu_out=None, after_proj=None, mid_hook=None, hT_pre=None):
        if hT_pre is None:
            hT_, hTk = stage_th(np_, h_, hk)
            yield
            if OPT['YIELD_TH']:
                yield
        else:
            hT_, hTk = hT_pre

        def proj(c):
            z_, zk = zp.next()
            for k in range(8):
                pe(lambda e, k=k: e.matmul(z_[0:np_, :], lhsT=hT_[:, k, 0:np_], rhs=win[:, k, c * 512:(c + 1) * 512],
                                           start=(k == 0), stop=(k == 7)), r=[hTk, ("win", c)], w=[zk])
            if after_proj is not None:
                after_proj(c)
            return z_, zk

        lag = OPT['LAG']
        z_, zk = proj(C_Q)
        rq_, rqk, rsq_ap, rsqk = gn_stats(z_, zk, np_, 8, 64)
        q_, qk = qn.next()

        def apply_q():
            gn_apply(dve, rq_, rqk, rsq_ap, rsqk, np_, 8, 64, q_[0:np_, :], [qk])
        if not lag:
            apply_q()
        yield
        if lag:
            apply_q()
        z_, zk = proj(C_K)
        rk_, rkk, rsk_ap, rskk = gn_stats(z_, zk, np_, 8, 64)
        kb_, kbk = knb.next()

        def apply_k():
            gn_apply(dve, rk_, rkk, rsk_ap, rskk, np_, 8, 64, kb_[0:np_, :], [kbk])
            gn_apply(pool, rk_, rkk, rsk_ap, rskk, np_, 8, 64, rk_[0:np_, :], [rkk])
            pool(lambda e: e.tensor_tensor(out=rk_[0:np_, :].rearrange("p (a b) -> p a b", b=64),
                                           in0=rk_[0:np_, :].rearrange("p (a b) -> p a b", b=64),
                                           in1=small[0:np_, 1, :].unsqueeze(1).to_broadcast([np_, 8, 64]), op=ALU.mult),
                 r=[rkk, ("small", 1)], w=[rkk])
            P.dma("sp", k_out, rk_[0:np_, :], reads=[rkk], final=True)
        if not lag:
            apply_k()
        yield
        if lag:
            apply_k()
        z_, zk = proj(C_V)
        v_, vk = vf.next()
        if copy_on_dve[0]:
            dve(lambda e: e.tensor_copy(v_[0:np_, :], z_[0:np_, :]), r=[zk], w=[vk])
        else:
            act(lambda e: e.activation(out=v_[0:np_, :], in_=z_[0:np_, :], func=AF.Copy), r=[zk], w=[vk])
        P.dma("sp", v_out, v_[0:np_, :], reads=[vk], final=True)
        va_ap, va_keys = va_dst[0], va_dst[1]
        if len(va_dst) > 2 and va_dst[2]:
            dve(lambda e: e.tensor_tensor(out=va_ap, in0=v_[0:np_, :].rearrange("p (h d) -> p h d", d=128),
                                          in1=vsc[0:np_, :].unsqueeze(2).to_broadcast([np_, NH, 128]), op=ALU.mult),
                r=[vk, "vsc"], w=va_keys)
        else:
            dve(lambda e: e.tensor_copy(va_ap, v_[0:np_, :].rearrange("p (h d) -> p h d", d=128)), r=[vk], w=va_keys)
        yield
        z_, zk = proj(C_GA)
        ta_, tak = ta.next()
        act(lambda e: e.activation(out=ta_[0:np_, :], in_=z_[0:np_, :], func=AF.Tanh, scale=0.5), r=[zk], w=[tak])
        dve(lambda e: e.scalar_tensor_tensor(out=ta_[0:np_, :], in0=ta_[0:np_, :], scalar=1.0, in1=z_[0:np_, :],
                                             op0=ALU.add, op1=ALU.mult), r=[tak, zk], w=[tak])
        yield
        z_, zk = proj(C_VA)
        rv_, rvk, rsv_ap, rsvk = gn_stats(z_, zk, np_, 4, 128)
        if sgu_out is not None:
            P.dma("sp", sgu_out, rv_[0:np_, :], reads=[rvk], final=True)
        vn_, vnk = van.next()

        def apply_va():
            gn_apply(pool if OPT['VA_POOL'] else dve, rv_, rvk, rsv_ap, rsvk, np_, 4, 128, rv_[0:np_, :], [rvk])
            pool(lambda e: e.tensor_tensor(out=vn_[0:np_, :], in0=rv_[0:np_, :], in1=gS[0:np_, :], op=ALU.mult),
                 r=[rvk, "gS"], w=[vnk])
        if not lag:
            apply_va()
        yield
        if lag:
            apply_va()
        z_, zk = proj(C_U)
        dve(lambda e: e.tensor_tensor(out=ta_[0:np_, :], in0=z_[0:np_, :], in1=ta_[0:np_, :], op=ALU.mult),
            r=[zk, tak], w=[tak])
        yield
        z_, zk = proj(C_GB)
        tb_, tbk = tb.next()
        act(lambda e: e.activation(out=tb_[0:np_, :], in_=z_[0:np_, :], func=AF.Tanh, scale=0.5), r=[zk], w=[tbk])
        dve(lambda e: e.scalar_tensor_tensor(out=tb_[0:np_, :], in0=tb_[0:np_, :], scalar=1.0, in1=z_[0:np_, :],
                                             op0=ALU.add, op1=ALU.mult), r=[tbk, zk], w=[tbk])
        pool(lambda e: e.tensor_tensor(out=tb_[0:np_, :].rearrange("p (a b) -> p a b", b=128),
                                       in0=tb_[0:np_, :].rearrange("p (a b) -> p a b", b=128),
                                       in1=gl[0:np_, :].unsqueeze(1).to_broadcast([np_, NH, 128]), op=ALU.mult),
             r=[tbk, "gl"], w=[tbk])
        yield
        if mid_hook is not None:
            mid_hook()
        if OPT['YIELD_QT']:
            yield
        tpt, tpk = tp.next()
        for h in range(NH):
            pe(lambda e, h=h: e.transpose(tpt[:, h * np_:(h + 1) * np_], q_[0:np_, h * 128:(h + 1) * 128],
                                          ident[0:np_, 0:np_]), r=[qk, "ident"], w=[tpk])
        QT_, QTk = QT.next()
        evac(QT_[:, :, 0:np_], tpt[:, 0:NH * np_].rearrange("p (h t) -> p h t", t=np_), [tpk], [QTk],
             scale_ap=gqc[:, 0:1], scale_key="gqc")
        yield
        tpt, tpk = tp.next()
        for h in range(NH):
            pe(lambda e, h=h: e.transpose(tpt[:, h * np_:(h + 1) * np_], kb_[0:np_, h * 128:(h + 1) * 128],
                                          ident[0:np_, 0:np_]), r=[kbk, "ident"], w=[tpk])
        kt_ap, kt_keys = kt_dst
        evac(kt_ap, tpt[:, 0:NH * np_].rearrange("p (h t) -> p h t", t=np_), [tpk], kt_keys,
             scale_ap=gkc[:, 0:1], scale_key="gkc")
        yield
        sg_, sgk = zp.next()
        for h in range(NH):
            pe(lambda e, h=h: e.matmul(sg_[0:np_, h * 128:(h + 1) * 128], lhsT=WT[0:np_, h, 0:np_],
                                       rhs=vn_[0:np_, h * 128:(h + 1) * 128], start=True, stop=True),
               r=[vnk, "WT"], w=[sgk])
        y_, yk = yb.next()
        for h in range(NH):
            dve(lambda e, h=h: e.scalar_tensor_tensor(out=y_[0:np_, h * 128:(h + 1) * 128],
                                                      in0=sg_[0:np_, h * 128:(h + 1) * 128], scalar=bS[0:np_, h:h + 1],
                                                      in1=ta_[0:np_, h * 128:(h + 1) * 128], op0=ALU.add, op1=ALU.mult),
                r=[sgk, "bS", tak], w=[(yk, "a")])
        state = dict(QT=QT_, QTk=QTk, yb=y_, ybk=yk, tb=tb_, tbk=tbk)
        yield state

    def finalize_head(O_, Ok, o_, ok, h, np_, nsum=1):
        r_, rk = r2.next()
        if nsum == 1:
            dve(lambda e: e.reciprocal(out=r_[0:np_, 0:2],
                                       in_=O_[0:np_, :].rearrange("p (c n) -> p c n", c=2)[:, :, 128]), r=[Ok], w=[rk])
        else:
            dve(lambda e: e.tensor_reduce(out=r_[0:np_, 0:2],
                                          in_=O_[0:np_, :].rearrange("p (c n) -> p c n", c=2)[:, :, 128:128 + nsum],
                                          axis=AX.X, op=ALU.add), r=[Ok], w=[rk])
            dve(lambda e: e.reciprocal(out=r_[0:np_, 0:2], in_=r_[0:np_, 0:2]), r=[rk], w=[rk])
        t_, tk = t1.next()
        dve(lambda e: e.tensor_scalar(out=t_[0:np_, :], in0=O_[0:np_, 256:384], scalar1=r_[0:np_, 1:2],
                                      scalar2=nlam[0:np_, 0:1], op0=ALU.mult, op1=ALU.mult),
            r=[Ok, rk, "nlam"], w=[tk])
        dve(lambda e: e.scalar_tensor_tensor(out=o_[0:np_, h * 128:(h + 1) * 128], in0=O_[0:np_, 0:128],
                                             scalar=r_[0:np_, 0:1], in1=t_[0:np_, :], op0=ALU.mult, op1=ALU.add),
            r=[Ok, rk, tk], w=[(ok, h)])

    def stage_c(l, np_, st, o_, ok, X, Xk, x_store):
        y_, yk, tb_, tbk = st["yb"], st["ybk"], st["tb"], st["tbk"]
        okeys = [(ok, h) for h in range(NH)]
        s_, sk = sq.next()
        act(lambda e: e.activation(out=s_[0:np_, :], in_=o_[0:np_, :], func=AF.Square), r=okeys, w=[sk])
        st_, stk = stat.next()
        dve(lambda e: e.tensor_reduce(out=st_[0:np_, 0:NH], in_=s_[0:np_, :].rearrange("p (a b) -> p a b", b=128),
                                      axis=AX.X, op=ALU.add), r=[sk], w=[stk])
        rs_ap, rsk = rsqrt_cols(st_[0:np_, 0:NH], NH, np_, 1.0 / 128, [stk])
        if OPT['LAG']:
            yield
        ceng = pool if OPT['C_POOL'] else dve
        ceng(lambda e: e.tensor_tensor(out=o_[0:np_, :].rearrange("p (a b) -> p a b", b=128),
                                       in0=o_[0:np_, :].rearrange("p (a b) -> p a b", b=128),
                                       in1=rs_ap.unsqueeze(2).to_broadcast([np_, NH, 128]), op=ALU.mult),
             r=okeys + [rsk], w=okeys)
        ceng(lambda e: e.tensor_tensor(out=y_[0:np_, 512:1024], in0=o_[0:np_, :], in1=tb_[0:np_, :], op=ALU.mult),
             r=okeys + [tbk], w=[(yk, "b")])
        yield
        if OPT['YIELD_C0']:
            yield
        tpt, tpk = tp.next()
        for k in range(8):
            pe(lambda e, k=k: e.transpose(tpt[:, k * np_:(k + 1) * np_], y_[0:np_, k * 128:(k + 1) * 128],
                                          ident[0:np_, 0:np_]), r=[(yk, "a"), (yk, "b"), "ident"], w=[tpk])
        yT_, yTk = yT.next()
        dve(lambda e: e.tensor_copy(yT_[:, :, 0:np_], tpt[:, 0:8 * np_].rearrange("p (k t) -> p k t", t=np_)),
            r=[tpk], w=[yTk])
        yield
        if OPT['YIELD_YT']:
            yield
        xo_, xok = xo.next()
        for n in range(2):
            z_, zk = zp.next()
            for k in range(8):
                pe(lambda e, k=k, n=n: e.matmul(z_[0:np_, :], lhsT=yT_[:, k, 0:np_],
                                                rhs=wout[:, k, n * 512:(n + 1) * 512], start=(k == 0), stop=(k == 7)),
                   r=[yTk, ("wout", n)], w=[zk])
            dve(lambda e, n=n: e.tensor_tensor(out=xo_[0:np_, n * 512:(n + 1) * 512], in0=z_[0:np_, :],
                                               in1=X[:, n * 512:(n + 1) * 512], op=ALU.add),
                r=[zk, Xk], w=[xok])
            yield
        x_store(xo_[0:np_, :], xok)
        yield

    def prompt_n(l, t, box):
        X_, Xk = xt.next()
        src_ = xp if l == 0 else xmid
        P.dma("sp", X_[:], src_[t * 128:(t + 1) * 128, :], reads=[("xmid", t)] if l > 0 else [], writes=[Xk])
        h_, hk, fin = stage_n(128, X_[:], Xk)
        box.update(X=X_[:], Xk=Xk, h=h_, hk=hk)
        yield
        if fin is not None:
            fin()
            yield
        if OPT['TH_IN_N']:
            box["hT"] = stage_th(128, h_, hk)
            yield

    def prompt_a(l, t, box, after_proj=None, mid_hook=None):
        kt_dst = (KT[:, :, t * 128:(t + 1) * 128], [("KT", t)])
        va_dst = (VA[:, t, :, 0:128], [("VA", t)], (OPT['PAIR'] and t % 2 == 1))
        g = stage_a(l, 128, box["h"], box["hk"], kt_dst, va_dst,
                    nkp[l, t * 128:(t + 1) * 128, :], nvp[l, t * 128:(t + 1) * 128, :], after_proj=after_proj,
                    mid_hook=mid_hook, hT_pre=box.get("hT"))
        late = (t - 1) >= OPT['COPY_DVE_T']
        early = (t - 1) < OPT['EVAC_ACT_T']
        while True:
            copy_on_dve[0] = late
            evac_on_act[0] = early
            try:
                r = next(g)
            except StopIteration:
                break
            finally:
                copy_on_dve[0] = False
                evac_on_act[0] = False
            if isinstance(r, dict):
                box["st"] = r
            yield

    def prompt_b(l, t, box):
        st = box["st"]
        QT_, QTk = st["QT"], st["QTk"]
        o_, ok = ob.next()
        groups = []
        for h in range(NH):
            j = 0
            while j <= t:
                if OPT['PAIR'] and j % 2 == 0 and j + 1 <= t:
                    groups.append([(h, j), (h, j + 1)])
                    j += 2
                else:
                    groups.append([(h, j)])
                    j += 1
        units = []
        cur, n = [], 0
        for g in groups:
            if n + len(g) > 4:
                units.append(cur)
                cur, n = [], 0
            cur.append(g)
            n += len(g)
        if cur:
            units.append(cur)
        Obuf = {}
        defer = []

        def flush():
            while defer:
                O_, Ok, h = defer.pop(0)
                finalize_head(O_, Ok, o_, ok, h, 128, nsum=3)

        def qk_unit(unit):
            S_, Sk = su.next()
            s = 0
            for g in unit:
                for (h, j) in g:
                    last = (j != t)
                    pe(lambda e, s=s, h=h, j=j, last=last: e.matmul(
                        S_[:, 0, s * 128:(s + 1) * 128], lhsT=KT[0:64, h, j * 128:(j + 1) * 128],
                        rhs=QT_[0:64, h, :], start=True, stop=last), r=[("KT", j), QTk], w=[Sk])
                    pe(lambda e, s=s, h=h, j=j, last=last: e.matmul(
                        S_[:, 1, s * 128:(s + 1) * 128], lhsT=KT[64:128, h, j * 128:(j + 1) * 128],
                        rhs=QT_[64:128, h, :], start=True, stop=last), r=[("KT", j), QTk], w=[Sk])
                    if j == t:
                        pe(lambda e, s=s, h=h: e.matmul(S_[:, 0, s * 128:(s + 1) * 128], lhsT=ident[:],
                                                        rhs=Dt[:, h, 0:128], start=False, stop=True),
                           r=["ident", "Dt"], w=[Sk])
                        pe(lambda e, s=s, h=h: e.matmul(S_[:, 1, s * 128:(s + 1) * 128], lhsT=ident[:],
                                                        rhs=Dt[:, h, 0:128], start=False, stop=True),
                           r=["ident", "Dt"], w=[Sk])
                    s += 1
            return S_, Sk

        def exp_unit(unit, S_, Sk):
            pts = []
            s = 0
            for g in unit:
                h, j0 = g[0]
                w_ = 128 * len(g)
                p_, pk = PT.next()
                act(lambda e, s=s, h=h, j0=j0, p_=p_, w_=w_: e.activation(
                    out=p_[:, :, 0:w_], in_=S_[:, :, s * 128:s * 128 + w_], func=AF.Exp,
                    bias=btab[:, h, (t - j0):(t - j0) + 1]), r=[Sk, "btab"], w=[pk])
                pts.append((p_, pk))
                s += len(g)
            return pts

        def av_unit(unit, pts):
            for g, (p_, pk) in zip(unit, pts):
                for i, (h, j) in enumerate(g):
                    if j == 0:
                        if len(defer) >= 2:
                            flush()
                        Obuf[h] = opb.next()
                    O_, Ok = Obuf[h]
                    pe(lambda e, h=h, j=j, i=i, p_=p_, O_=O_: e.matmul(
                        O_[:, 0:131], lhsT=p_[:, 0, i * 128:(i + 1) * 128], rhs=VA[:, j, h, 0:131],
                        start=(j == 0), stop=(j == t)), r=[pk, ("VA", j)], w=[Ok])
                    pe(lambda e, h=h, j=j, i=i, p_=p_, O_=O_: e.matmul(
                        O_[:, 256:387], lhsT=p_[:, 1, i * 128:(i + 1) * 128], rhs=VA[:, j, h, 0:131],
                        start=False, stop=(j == t), skip_group_check=True), r=[pk, ("VA", j)], w=[Ok])
                    if j == t:
                        if OPT['LAG']:
                            defer.append((O_, Ok, h))
                        else:
                            finalize_head(O_, Ok, o_, ok, h, 128, nsum=3)

        pend = []
        for unit in units:
            flush()
            S_, Sk = qk_unit(unit)
            pend.append((unit, exp_unit(unit, S_, Sk)))
            if len(pend) > 2:
                av_unit(*pend.pop(0))
            yield
        while pend:
            flush()
            av_unit(*pend.pop(0))
            yield
        flush()

        box.update(o=o_, ok=ok)

    def prompt_c(l, t, box):
        def x_store(xo_ap, xok):
            if l == L - 1:
                P.dma("sp", yp[t * 128:(t + 1) * 128, :], xo_ap, reads=[xok], final=True)
            else:
                P.dma("sp", xmid[t * 128:(t + 1) * 128, :], xo_ap, reads=[xok], writes=[("xmid", t)])

        for _ in stage_c(l, 128, box["st"], box["o"], box["ok"], box["X"], box["Xk"], x_store):
            yield

    def cache_load(l, b, h, box):
        kc_, kck = Kc.next()
        vs_, vsk = Vs.next()
        P.dma("pool", kc_[:], ck[l, b].rearrange("(j p) n -> p j n", p=128)[:, :, h * 128:(h + 1) * 128],
              writes=[kck])
        P.dma("pool", vs_[:, :, 0:128], cv[l, b].rearrange("(j p) n -> p j n", p=128)[:, :, h * 128:(h + 1) * 128],
              writes=[vsk])
        box[("cache", h)] = (kc_, kck, vs_, vsk)

    def sample_n(l, b, box):
        np_ = DEC
        Xt_, Xk = xt.next()
        X = Xt_[0:np_, :]
        if l == 0:
            P.dma("sp", X, xs[b], writes=[Xk])
        else:
            P.dma("sp", X, xsmid[b], reads=[("xsmid", b)], writes=[Xk])
        h_, hk, fin = stage_n(np_, X, Xk)
        box.update(X=X, Xk=Xk, h=h_, hk=hk)
        yield
        if fin is not None:
            fin()
            yield
        if OPT['TH_IN_N']:
            box["hT"] = stage_th(np_, h_, hk)
            yield

    def sample_a(l, b, box, after_proj=None):
        np_ = DEC
        kt_, ktk = KTn.next()
        vn_, vnk = Vn.next()
        box.update(kt=kt_, ktk=ktk, vn=vn_, vnk=vnk)
        g = stage_a(l, np_, box["h"], box["hk"], (kt_[:, :, :], [ktk]), (vn_[:, :, 0:128], [vnk]), nks[l, b], nvs[l, b],
                    sgu_out=nsg[l, b], after_proj=after_proj, hT_pre=box.get("hT"))
        for i, r in enumerate(g):
            if isinstance(r, dict):
                box["st"] = r
            yield

    def sample_b(l, b, box):
        np_ = DEC
        st = box["st"]
        kt_, ktk, vn_, vnk = box["kt"], box["ktk"], box["vn"], box["vnk"]
        QT_, QTk = st["QT"], st["QTk"]
        o_, ok = ob.next()
        cache_load(l, b, 0, box)
        cache_load(l, b, 1, box)
        for h in range(NH):
            kc_, kck, vs_, vsk = box[("cache", h)]
            tpt, tpk = tp.next()
            for j in range(8):
                pe(lambda e, j=j: e.transpose(tpt[:, j * 128:(j + 1) * 128], kc_[:, j, :], ident[:]),
                   r=[kck, "ident"], w=[tpk])
            kts_, ktsk = KTs.next()
            dve(lambda e: e.tensor_copy(kts_[:], tpt[:]), r=[tpk], w=[ktsk])
            yield
            sps, Sk_ = su.next()
            skeys = [Sk_]
            for j in range(8):
                for c in range(2):
                    pe(lambda e, j=j, c=c: e.matmul(sps[:, c, j * DEC:(j + 1) * DEC],
                                                    lhsT=kts_[c * 64:(c + 1) * 64, j * 128:(j + 1) * 128],
                                                    rhs=QT_[c * 64:(c + 1) * 64, h, 0:DEC], start=True, stop=True),
                       r=[ktsk, QTk], w=skeys)
            for c in range(2):
                pe(lambda e, c=c: e.matmul(sps[0:DEC, c, 128:128 + DEC], lhsT=kt_[c * 64:(c + 1) * 64, h, 0:DEC],
                                           rhs=QT_[c * 64:(c + 1) * 64, h, 0:DEC], start=True, stop=False),
                   r=[ktk, QTk], w=skeys)
            for c in range(2):
                pe(lambda e, c=c: e.matmul(sps[0:DEC, c, 128:128 + DEC], lhsT=ident[:, 0:DEC],
                                           rhs=Dt[:, h, 0:DEC], start=False, stop=True),
                   r=["ident", "Dt"], w=skeys)
            sb_, sbk = ssb.next()
            dve(lambda e: e.tensor_tensor(out=sb_[:], in0=sps[:, :, 0:128],
                                          in1=bsx0[:, h, :, :].rearrange("p j q -> p (j q)").unsqueeze(1).to_broadcast([128, 2, 128]),
                                          op=ALU.add), r=skeys + ["bsx0"], w=[sbk])
            p_, pk = PTs.next()
            act(lambda e: e.activation(out=p_[:], in_=sb_[:], func=AF.Exp, bias=negM[:, 0:1]), r=[sbk, "negM"], w=[pk])
            pn_, pnk = PTn.next()
            act(lambda e: e.activation(out=pn_[:], in_=sps[0:DEC, :, 128:128 + DEC], func=AF.Exp,
                                       bias=btab[0:DEC, h, 0:1]), r=skeys + ["btab"], w=[pnk])
            yield
            O_, Ok = opb.next()
            for c in range(2):
                for j in range(8):
                    pe(lambda e, j=j, c=c: e.matmul(O_[0:DEC, c * 256:c * 256 + 129], lhsT=p_[:, c, j * DEC:(j + 1) * DEC],
                                                    rhs=vs_[:, j, 0:129], start=(c == 0 and j == 0), stop=False,
                                                    skip_group_check=True), r=[pk, vsk], w=[Ok])
                pe(lambda e, c=c: e.matmul(O_[0:DEC, c * 256:c * 256 + 129], lhsT=pn_[:, c, :], rhs=vn_[:, h, 0:129],
                                           start=False, stop=True, skip_group_check=True), r=[pnk, vnk], w=[Ok])
            finalize_head(O_, Ok, o_, ok, h, DEC)
            if h + 2 < NH:
                cache_load(l, b, h + 2, box)
            yield

        box.update(o=o_, ok=ok)

    def sample_c(l, b, box):
        def x_store(xo_ap, xok):
            if l == L - 1:
                P.dma("sp", ys[b], xo_ap, reads=[xok], final=True)
            else:
                P.dma("sp", xsmid[b], xo_ap, reads=[xok], writes=[("xsmid", b)])

        for _ in stage_c(l, DEC, box["st"], box["o"], box["ok"], box["X"], box["Xk"], x_store):
            yield

    def run(g):
        for _ in g:
            pass

    def zipgen(gens):
        gl_ = [g for g, n in gens]
        perm = {0: None, 1: (1, 0, 2, 3), 2: (2, 0, 1, 3), 3: (3, 0, 1, 2), 4: (0, 2, 1, 3), 5: (3, 1, 0, 2)}[OPT['ORDER']]
        if perm is not None and len(gl_) == 4:
            gl_ = [gl_[p] for p in perm]
        while gl_:
            for g in list(gl_):
                try:
                    next(g)
                except StopIteration:
                    gl_.remove(g)

    def load_w_in(l):
        src_ = w_in[l].rearrange("(kt p) n -> p kt n", p=128)
        for c in CHUNK_ORDER:
            P.dma("pool", win[:, :, c * 512:(c + 1) * 512], src_[:, :, c * 512:(c + 1) * 512],
                  writes=[("win", c)])

    def load_w_out(l):
        src2 = w_out[l].rearrange("(kt p) n -> p kt n", p=128)
        for n in range(2):
            P.dma("pool", wout[:, :, n * 512:(n + 1) * 512], src2[:, :, n * 512:(n + 1) * 512],
                  writes=[("wout", n)])

    tiles = []
    for l in range(L):
        for t in range(NT):
            tiles.append((l, "p", t))
        for b in range(NS):
            tiles.append((l, "s", b))
    boxes = [dict() for _ in tiles]

    def gen_n(i):
        l, kind, x = tiles[i]
        return prompt_n(l, x, boxes[i]) if kind == "p" else sample_n(l, x, boxes[i])

    def gen_a(i):
        l, kind, x = tiles[i]
        hook = None
        if i + 1 < len(tiles) and tiles[i + 1][0] != l:
            src_ = w_in[l + 1].rearrange("(kt p) n -> p kt n", p=128)

            def hook(c):
                P.dma("pool", win[:, :, c * 512:(c + 1) * 512], src_[:, :, c * 512:(c + 1) * 512],
                      writes=[("win", c)])
        mid = None
        if i == 0 or tiles[i - 1][0] != l:
            mid = (lambda l=l: load_params_a2(l))
        return prompt_a(l, x, boxes[i], hook, mid) if kind == "p" else sample_a(l, x, boxes[i], hook)

    def gen_b(i):
        l, kind, x = tiles[i]
        return prompt_b(l, x, boxes[i]) if kind == "p" else sample_b(l, x, boxes[i])

    def gen_c(i):
        l, kind, x = tiles[i]
        return prompt_c(l, x, boxes[i]) if kind == "p" else sample_c(l, x, boxes[i])

    def layer_of(i):
        return tiles[i][0] if 0 <= i < len(tiles) else None

    NTT = len(tiles)
    load_params_n(0)
    setup_min()
    run(gen_n(0))
    if NTT > 1:
        run(gen_n(1))
    load_w_in(0)
    setup_consts()
    load_params_a(0)
    load_params_b(0)
    load_w_out(0)
    run(gen_a(0))
    for i in range(NTT + 1):
        gens = []
        if i < NTT:
            l_, kind_, x_ = tiles[i]
            nb = (x_ + 1) + 1 if kind_ == "p" else 3 * NH
            gens.append((gen_b(i), nb))
        if i + 1 < NTT:
            if layer_of(i + 1) != layer_of(i):
                load_params_a(layer_of(i + 1))
            gens.append((gen_a(i + 1), 11))
        if i >= 1:
            gens.append((gen_c(i - 1), 5))
        if i + 2 < NTT:
            if layer_of(i + 2) != layer_of(i + 1):
                load_params_n(layer_of(i + 2))
            gens.append((gen_n(i + 2), 1))
        zipgen(gens)
        if 1 <= i < NTT and layer_of(i) != layer_of(i - 1):
            load_w_out(layer_of(i))
        if i + 1 < NTT and layer_of(i + 1) != layer_of(i):
            load_params_b(layer_of(i + 1))

    sems = {e: es.enter_context(nc.semaphore("s_" + e)) for e in ENGS}
    dsems = {e: [es.enter_context(nc.semaphore(f"d_{e}{i}")) for i in range(NDMA_SEM)] for e in ("sp", "pool")}
    for e in ENGS:
        dsems.setdefault(e, [])
    with nc.Block() as block:
        @block.tensor
        def _(e):
            P.emit_engine("pe", e, sems, dsems)

        @block.scalar
        def _(e):
            P.emit_engine("act", e, sems, dsems)

        @block.vector
        def _(e):
            P.emit_engine("dve", e, sems, dsems)

        @block.gpsimd
        def _(e):
            P.emit_engine("pool", e, sems, dsems)

        @block.sync
        def _(e):
            P.emit_engine("sp", e, sems, dsems)
    es.close()
    return nc, P


_CACHE = {}


def _get_program(T, L, NS):
    key = (T, L, NS)
    if key not in _CACHE:
        _CACHE[key] = build_program(T, L, NS)[0]
    return _CACHE[key]


def kernel(x_prompt, x_sample, cache_k, cache_v, norm_g, w_in, sgu_norm_g, sgu_w, sgu_b,
           q_norm_g, k_norm_g, lambda_q1, lambda_k1, lambda_q2, lambda_k2, subln_g, w_out):
    f = lambda a: np.ascontiguousarray(np.asarray(a, dtype=np.float32))
    x_prompt, x_sample, cache_k, cache_v = f(x_prompt), f(x_sample), f(cache_k), f(cache_v)
    B, T, _ = x_prompt.shape
    L = w_in.shape[0]
    n_cores = B
    NS = x_sample.shape[0] // n_cores
    nc = _get_program(T, L, NS)
    shared = {
        "norm_g": f(norm_g), "w_in": f(w_in), "sgu_norm_g": f(sgu_norm_g).reshape(L, 512), "sgu_w": f(sgu_w),
        "sgu_b": f(sgu_b), "q_norm_g": f(q_norm_g), "k_norm_g": f(k_norm_g), "lq1": f(lambda_q1),
        "lk1": f(lambda_k1), "lq2": f(lambda_q2), "lk2": f(lambda_k2), "subln_g": f(subln_g), "w_out": f(w_out),
    }
    ckr = cache_k.reshape(L, B * NS, PAST, 512)
    cvr = cache_v.reshape(L, B * NS, PAST, 512)
    in_maps = []
    for c in range(n_cores):
        m = dict(shared)
        m["xp"] = x_prompt[c]
        m["xs"] = x_sample[c * NS:(c + 1) * NS]
        m["ck"] = np.ascontiguousarray(ckr[:, c * NS:(c + 1) * NS])
        m["cv"] = np.ascontiguousarray(cvr[:, c * NS:(c + 1) * NS])
        in_maps.append(m)
    res = run_bass_kernel_spmd(nc, in_maps, core_ids=list(range(n_cores)))
    rs = res.results
    y_prompt = np.stack([r["yp"] for r in rs], axis=0)
    y_sample = np.concatenate([r["ys"] for r in rs], axis=0)
    nk_p = np.stack([r["nkp"] for r in rs], axis=1).reshape(L, B, T, NH, 2, 64)
    nv_p = np.stack([r["nvp"] for r in rs], axis=1).reshape(L, B, T, NH, 128)
    nk_s = np.concatenate([r["nks"] for r in rs], axis=1).reshape(L, B * NS, DEC, NH, 2, 64)
    nv_s = np.concatenate([r["nvs"] for r in rs], axis=1).reshape(L, B * NS, DEC, NH, 128)
    ns_g = np.concatenate([r["nsg"] for r in rs], axis=1).reshape(L, B * NS, DEC, 512)
    return (y_prompt.astype(np.float32), y_sample.astype(np.float32), nk_p.astype(np.float32),
            nv_p.astype(np.float32), nk_s.astype(np.float32), nv_s.astype(np.float32), ns_g.astype(np.float32))
```
